# Optimizing a Trainium2 kernel written in Bass

```python
import jax, jax.numpy as jnp
from jax import lax
import numpy as np

D_MODEL = 1024
BATCH = 8
SEQ = 2048
DEPTH = 4
DEC_BATCH = 128
DEC_SEQ = 8
PAST_LEN = 16384
PAGE_SIZE = 128

D_RNN = D_MODEL
N_RNN_BLOCKS = 16
RNN_BLOCK = D_RNN // N_RNN_BLOCKS
CONV_A_WIDTH = 4
C_LRU = 8.0
D_SC = D_MODEL
CONV_B_WIDTH = 3
PROJ_COLS = D_RNN + 3 * D_SC + 2 * D_MODEL
N_KEYS = 128
N_EXPERTS = N_KEYS * N_KEYS
PEER_HEADS = 8
D_KEY = 256
D_KEY_HALF = D_KEY // 2
TOPK = 16
PEER_BLOCK = 128
EPS = 1e-6

kernel_name = "hybrid_rglru_shortconv_peer_step"


def rmsnorm(x, g):
    xf = x.astype(jnp.float32)
    var = jnp.mean(xf * xf, axis=-1, keepdims=True)
    return (xf * lax.rsqrt(var + EPS)).astype(x.dtype) * g


def modulate(h, shift, scale):
    return h * (1.0 + scale[:, None, :]) + shift[:, None, :]


def causal_depthwise_conv(x, buf, w, b):
    width = w.shape[0]
    t = x.shape[1]
    xp = jnp.concatenate([buf.astype(x.dtype), x], axis=1)
    y = xp[:, 0:t] * w[0]
    for k in range(1, width):
        y = y + xp[:, k:k + t] * w[k]
    if b is not None:
        y = y + b
    return y, xp[:, -(width - 1):]


def block_diag_linear(x, w, b):
    bsz, t, c = x.shape
    xb = x.reshape(bsz, t, N_RNN_BLOCKS, RNN_BLOCK)
    y = jnp.einsum('btnj,njk->btnk', xb, w.astype(jnp.float32))
    return y.reshape(bsz, t, c) + b.astype(jnp.float32)


def rglru(xc, h0, is_prompt, w_a, b_a, w_x, b_x, lam):
    xf = xc.astype(jnp.float32)
    r = jax.nn.sigmoid(block_diag_linear(xf, w_a, b_a))
    i = jax.nn.sigmoid(block_diag_linear(xf, w_x, b_x))
    log_a = -C_LRU * r * jax.nn.softplus(-lam.astype(jnp.float32))
    a = jnp.exp(log_a)
    mult = jnp.sqrt(-jnp.expm1(2.0 * log_a))
    t = xc.shape[1]
    if is_prompt:
        first = (jnp.arange(t) == 0)[None, :, None]
        mult = jnp.where(first, 1.0, mult)
    bterm = mult * (i * xf)

    def combine(left, right):
        a1, b1 = left
        a2, b2 = right
        return a1 * a2, a2 * b1 + b2

    a_cum, b_cum = lax.associative_scan(combine, (a, bterm), axis=1)
    h = a_cum * h0.astype(jnp.float32)[:, None, :] + b_cum
    return h.astype(xc.dtype), h[:, -1].astype(xc.dtype)


def peer(h, w_q, sub_keys, u_tab, v_tab):
    bsz, t, d = h.shape
    hf = h.reshape(-1, d)
    n = hf.shape[0]
    pad = (-n) % PEER_BLOCK
    hp = jnp.pad(hf, ((0, pad), (0, 0))).reshape(-1, PEER_BLOCK, d)

    def block(xb):
        tb = xb.shape[0]
        q = (xb @ w_q).reshape(tb, PEER_HEADS, 2, D_KEY_HALF)
        s = jnp.einsum('thpd,hpkd->thpk', q, sub_keys).astype(jnp.float32)
        sv, si = lax.top_k(s, TOPK)
        cand = sv[:, :, 0, :, None] + sv[:, :, 1, None, :]
        cand_idx = si[:, :, 0, :, None] * N_KEYS + si[:, :, 1, None, :]
        cand = cand.reshape(tb, PEER_HEADS, TOPK * TOPK)
        cand_idx = cand_idx.reshape(tb, PEER_HEADS, TOPK * TOPK)
        fv, fi = lax.top_k(cand, TOPK)
        idx = jnp.take_along_axis(cand_idx, fi, axis=-1)
        g = jax.nn.softmax(fv, axis=-1)
        ue = u_tab[idx]
        act = jax.nn.gelu(jnp.einsum('thkd,td->thk', ue, xb).astype(jnp.float32), approximate=False)
        ve = v_tab[idx]
        return jnp.einsum('thk,thkd->td', (g * act).astype(xb.dtype), ve)

    out = lax.map(block, hp)
    return out.reshape(-1, d)[:n].reshape(bsz, t, d)


def trunk_layer(x, c, buf_a, h0, buf_b, is_prompt, w_ada, b_ada, norm1, norm2, w_in,
                conv_a_w, conv_a_b, rg_w_a, rg_b_a, rg_w_x, rg_b_x, rg_lambda,
                conv_b_w, w_out, peer_w_q, peer_sub_keys, peer_u, peer_v):
    mod = c @ w_ada + b_ada
    sh1, sc1, g1, sh2, sc2, g2 = jnp.split(mod, 6, axis=-1)
    hn = modulate(rmsnorm(x, norm1), sh1, sc1)
    proj = hn @ w_in
    xa = proj[..., :D_RNN]
    o = D_RNN
    bg = proj[..., o:o + D_SC]
    cg = proj[..., o + D_SC:o + 2 * D_SC]
    xs = proj[..., o + 2 * D_SC:o + 3 * D_SC]
    o = o + 3 * D_SC
    ga = proj[..., o:o + D_MODEL]
    gb = proj[..., o + D_MODEL:o + 2 * D_MODEL]
    xa_c, new_buf_a = causal_depthwise_conv(xa, buf_a, conv_a_w, conv_a_b)
    ya, h_last = rglru(xa_c, h0, is_prompt, rg_w_a, rg_b_a, rg_w_x, rg_b_x, rg_lambda)
    u = cg * xs
    uc, new_buf_b = causal_depthwise_conv(u, buf_b, conv_b_w, None)
    yb = bg * uc
    merged = jax.nn.sigmoid(ga) * ya + jax.nn.sigmoid(gb) * yb
    x = x + g1[:, None, :] * (merged @ w_out)
    hn2 = modulate(rmsnorm(x, norm2), sh2, sc2)
    x = x + g2[:, None, :] * peer(hn2, peer_w_q, peer_sub_keys, peer_u, peer_v)
    return x, new_buf_a, h_last, new_buf_b


def setup_inputs(seed: int = 0) -> dict:
    key = jax.random.key(seed)
    ks = jax.random.split(key, 32)
    D = D_MODEL
    f32 = jnp.float32

    def nrm(k, shape, s):
        return jax.random.normal(k, shape, f32) * s

    a0 = jax.random.uniform(ks[18], (DEPTH, D_RNN), f32, 0.9, 0.999)
    a_base = a0 ** (1.0 / C_LRU)
    rg_lambda = jnp.log(a_base) - jnp.log1p(-a_base)
    return {
        'x_prompt': nrm(ks[0], (BATCH, SEQ, D), 1.0),
        'x_sample': nrm(ks[1], (DEC_BATCH, DEC_SEQ, D), 1.0),
        'c_prompt': nrm(ks[2], (BATCH, D), 1.0),
        'c_sample': nrm(ks[3], (DEC_BATCH, D), 1.0),
        'state_conv_a': nrm(ks[4], (DEPTH, DEC_BATCH, CONV_A_WIDTH - 1, D_RNN), 1.0),
        'state_h': nrm(ks[5], (DEPTH, DEC_BATCH, D_RNN), 0.5),
        'state_conv_b': nrm(ks[6], (DEPTH, DEC_BATCH, CONV_B_WIDTH - 1, D_SC), 1.0),
        'w_ada': nrm(ks[7], (DEPTH, D, 6 * D), 0.5 * D ** -0.5),
        'b_ada': nrm(ks[8], (DEPTH, 6 * D), 0.02),
        'norm1': 1.0 + nrm(ks[9], (DEPTH, D), 0.02),
        'norm2': 1.0 + nrm(ks[10], (DEPTH, D), 0.02),
        'w_in': nrm(ks[11], (DEPTH, D, PROJ_COLS), D ** -0.5),
        'conv_a_w': nrm(ks[12], (DEPTH, CONV_A_WIDTH, D_RNN), CONV_A_WIDTH ** -0.5),
        'conv_a_b': nrm(ks[13], (DEPTH, D_RNN), 0.02),
        'rg_w_a': nrm(ks[14], (DEPTH, N_RNN_BLOCKS, RNN_BLOCK, RNN_BLOCK), RNN_BLOCK ** -0.5),
        'rg_b_a': nrm(ks[15], (DEPTH, D_RNN), 0.02),
        'rg_w_x': nrm(ks[16], (DEPTH, N_RNN_BLOCKS, RNN_BLOCK, RNN_BLOCK), RNN_BLOCK ** -0.5),
        'rg_b_x': nrm(ks[17], (DEPTH, D_RNN), 0.02),
        'rg_lambda': rg_lambda,
        'conv_b_w': nrm(ks[19], (DEPTH, CONV_B_WIDTH, D_SC), CONV_B_WIDTH ** -0.5),
        'w_out': nrm(ks[20], (DEPTH, D, D), D ** -0.5),
        'peer_w_q': nrm(ks[21], (DEPTH, D, PEER_HEADS * D_KEY), D ** -0.5),
        'peer_sub_keys': nrm(ks[22], (DEPTH, PEER_HEADS, 2, N_KEYS, D_KEY_HALF), D_KEY_HALF ** -0.5),
        'peer_u': nrm(ks[23], (DEPTH, N_EXPERTS, D), D ** -0.5),
        'peer_v': nrm(ks[24], (DEPTH, N_EXPERTS, D), 0.5 * PEER_HEADS ** -0.5),
        'final_norm': 1.0 + nrm(ks[25], (D,), 0.02),
    }


def reference(x_prompt, x_sample, c_prompt, c_sample, state_conv_a, state_h, state_conv_b,
              w_ada, b_ada, norm1, norm2, w_in, conv_a_w, conv_a_b, rg_w_a, rg_b_a,
              rg_w_x, rg_b_x, rg_lambda, conv_b_w, w_out, peer_w_q, peer_sub_keys,
              peer_u, peer_v, final_norm):
    xp = x_prompt
    xs = x_sample
    bp = x_prompt.shape[0]
    buf_a_p = jnp.zeros((bp, CONV_A_WIDTH - 1, D_RNN), x_prompt.dtype)
    buf_b_p = jnp.zeros((bp, CONV_B_WIDTH - 1, D_SC), x_prompt.dtype)
    h0_p = jnp.zeros((bp, D_RNN), x_prompt.dtype)
    ca_p, hh_p, cb_p, ca_s, hh_s, cb_s = [], [], [], [], [], []
    for l in range(DEPTH):
        w = (w_ada[l], b_ada[l], norm1[l], norm2[l], w_in[l], conv_a_w[l], conv_a_b[l],
             rg_w_a[l], rg_b_a[l], rg_w_x[l], rg_b_x[l], rg_lambda[l], conv_b_w[l],
             w_out[l], peer_w_q[l], peer_sub_keys[l], peer_u[l], peer_v[l])
        xp, na, nh, nb = trunk_layer(xp, c_prompt, buf_a_p, h0_p, buf_b_p, True, *w)
        ca_p.append(na)
        hh_p.append(nh)
        cb_p.append(nb)
        xs, na, nh, nb = trunk_layer(xs, c_sample, state_conv_a[l], state_h[l],
                                     state_conv_b[l], False, *w)
        ca_s.append(na)
        hh_s.append(nh)
        cb_s.append(nb)
    y_prompt = rmsnorm(xp, final_norm)
    y_sample = rmsnorm(xs, final_norm)
    return (y_prompt, y_sample, jnp.stack(ca_p, 0), jnp.stack(hh_p, 0), jnp.stack(cb_p, 0),
            jnp.stack(ca_s, 0), jnp.stack(hh_s, 0), jnp.stack(cb_s, 0))
```

```python
import numpy as np
import concourse.bass as bass
import concourse.mybir as mybir
from concourse.bass_utils import run_bass_kernel_spmd

F32 = mybir.dt.float32
BF16 = mybir.dt.bfloat16
I32 = mybir.dt.int32
U32 = mybir.dt.uint32
ALU = mybir.AluOpType
AF = mybir.ActivationFunctionType
AX = mybir.AxisListType

L = 4
D = 1024
NT = 2176
NTILE = 17
NCH = 8
EPS = 1e-6
SEM_LIMIT = 60000
SB_BASE = 16512
SB_END = 229376
DENSE = False


class Buf:
    __slots__ = ("name", "writer", "readers")

    def __init__(self, name=""):
        self.name = name
        self.writer = None
        self.readers = {}


class Sched:
    def __init__(self, nc):
        self.nc = nc
        self.eng = {"pe": nc.tensor, "act": nc.scalar, "dve": nc.vector,
                    "pool": nc.gpsimd, "sp": nc.sync}
        self.esem = {}
        self.ecnt = {}
        self.waited = {k: {} for k in self.eng}
        self.nsem = 0
        self.dkeys = {}
        self.ninst = {k: 0 for k in self.eng}
        for k in ("pe", "act", "dve", "pool"):
            self._new_esem(k)

    def _alloc_sem(self, name):
        self.nsem += 1
        return self.nc.alloc_semaphore(name)

    def _new_esem(self, k):
        self.esem[k] = self._alloc_sem("e_%s_%d" % (k, self.nsem))
        self.ecnt[k] = 0

    def _wait(self, e, dep):
        sem, val, src = dep[0], dep[1], dep[2]
        if len(dep) > 3 and dep[3][0] is sem:
            val = 16 * dep[3][1]
        w = self.waited[e]
        if w.get(id(sem), 0) >= val:
            return
        if src == e and sem is self.esem[e]:
            val = max(val, self.ecnt[e] - 3)
        self.eng[e].wait_ge(sem, val)
        self.ninst[e] += 1
        w[id(sem)] = val

    def _deps(self, e, reads, writes):
        for b in reads:
            if b.writer is not None:
                if b.writer[2] == e and e == "pe":
                    continue
                self._wait(e, b.writer)
        for b in writes:
            if b.writer is not None and not b.readers and not (b.writer[2] == e and e == "pe"):
                self._wait(e, b.writer)
            for r in b.readers.values():
                if not (r[2] == e and e == "pe"):
                    self._wait(e, r)

    def op(self, e, meth, reads, writes, *a, soft=(), **k):
        self._deps(e, list(reads) + list(soft), writes)
        if self.ecnt[e] >= SEM_LIMIT:
            self._new_esem(e)
        ins = getattr(self.eng[e], meth)(*a, **k)
        self.ecnt[e] += 1
        self.ninst[e] += 1
        ins.then_inc(self.esem[e], 1)
        tok = (self.esem[e], self.ecnt[e], e)
        for b in reads:
            b.readers[e] = tok
        for b in writes:
            b.writer = tok
            b.readers = {}
        return ins

    def dma(self, q, key, reads, writes, meth="dma_start", **k):
        self._deps(q, reads, writes)
        st = self.dkeys.get(key)
        if st is None or st[1] >= 3700:
            st = [self._alloc_sem("d_%s_%d" % (key, self.nsem)), 0]
            self.dkeys[key] = st
        ins = getattr(self.eng[q], meth)(**k)
        self.ninst[q] += 1
        st[1] += 1
        ins.then_inc(st[0], 16)
        tok = (st[0], 16 * st[1], "dma:" + key, st)
        for b in reads:
            b.readers["dma:" + key] = tok
        for b in writes:
            b.writer = tok
            b.readers = {}
        return ins

    def wait_all(self, e, bufs):
        for b in bufs:
            if b.writer is not None:
                self._wait(e, b.writer)
            for r in b.readers.values():
                self._wait(e, r)

    def barrier(self):
        toks = []
        for k in ("pe", "act", "dve", "pool"):
            if self.ecnt[k] > 0:
                toks.append((self.esem[k], self.ecnt[k], k))
        for key, st in self.dkeys.items():
            toks.append((st[0], 16 * st[1], "dma:" + key))
        for e in self.eng:
            for t in toks:
                if t[2] == e:
                    continue
                self._wait(e, t)


class Arena:
    def __init__(self, nc, base, end):
        self.nc = nc
        self.off = base
        self.end = end

    def __call__(self, name, shape, dt=F32):
        n = 1
        for s in shape[1:]:
            n *= s
        n *= 2 if dt == BF16 else 4
        h = self.nc.alloc_sbuf_tensor_at(name, list(shape), dt, offset=self.off)
        self.off += (n + 63) // 64 * 64
        assert self.off <= self.end, ("SBUF overflow", name, self.off, self.end)
        return h


def bufs(n, name):
    return [Buf("%s%d" % (name, i)) for i in range(n)]


def build_program(n_layers=L, do_peer=True, dbg=False, dense=DENSE):
    nc = bass.Bass("TRN2", target_bir_lowering=False)
    S = Sched(nc)

    def din(n, s, dt=F32):
        return nc.dram_tensor(n, s, dt, kind="ExternalInput").ap()

    def dout(n, s, dt=F32):
        return nc.dram_tensor(n, s, dt, kind="ExternalOutput").ap()

    xin = din("xin", [NT, D])
    cin = din("cin", [17, D])
    state_d = din("state", [L, 96, D])
    w_ada = din("w_ada", [L, D, 6 * D])
    w_in = din("w_in", [L, D, 6 * D])
    w_out = din("w_out", [L, D, D])
    w_q = din("w_q", [L, D, 2048])
    skT_d = din("skT", [L, 16, 128, 128])
    pA_d = din("pA", [L, 128, 128])
    pB_d = din("pB", [L, 32, 128])
    wbda_d = din("wbda", [L, 8, 128, 128])
    wbdx_d = din("wbdx", [L, 8, 128, 128])
    ident_d = din("ident", [128, 128])
    iota_d = din("iota16", [128, 16])
    if dense:
        ut_d = [din("ut%d" % l, [D, 16384]) for l in range(L)]
        iota128_d = din("iota128", [128, 128])
        Wd = [nc.dram_tensor("Wd%d" % l, [NT // 32, 128, 128, 32], BF16, kind="Internal").ap() for l in range(L)]
        b_wd = Buf("wd")
    else:
        u_d = [din("u%d" % l, [16384, D]) for l in range(L)]
    v_d = [din("v%d" % l, [16384, D]) for l in range(L)]
    uvb_d = [nc.dram_tensor("uvb%d" % l, [16384, 2 * D], BF16, kind="Internal").ap() for l in range(L)]
    b_uv = bufs(L, "uv")
    modD = nc.dram_tensor("modD", [128, L * 48 * 17], F32, kind="Internal").ap()
    b_modD = Buf("modD")
    y_d = dout("y", [NT, D])
    ost_d = dout("ost", [L, 102, D])
    O_bufs = []

    PS = nc.alloc_psum_tensor("PS", [128, 4096], F32)
    BK = bufs(8, "bank")

    def bank(b, n=512, p=128):
        return PS[0:p, b * 512:b * 512 + n]

    A = Arena(nc, SB_BASE, SB_END)
    xT = A("xT", [128, NCH, NT])
    b_x = bufs(NTILE, "x")
    modL = A("modL", [128, 48, 17])
    b_mod = Buf("mod")
    b_modall = Buf("modall")
    PT_A = A("PT_A", [128, L, 128])
    PT_B = A("PT_B", [128, L, 32])
    b_pt = Buf("pt")
    ident = A("ident", [128, 128])
    b_ident = Buf("ident")
    ones = A("ones", [128, 128])
    b_ones = Buf("ones")
    iota = A("iota", [128, 16])
    b_iota = Buf("iota")
    negsp8 = A("negsp8", [128, L, 8])
    b_nsp = Buf("nsp")
    cT = A("cT", [128, NCH, 17])
    b_cT = Buf("cT")
    gs1p = A("gs1p", [128, 8])
    gs2p = A("gs2p", [128, 8])
    gs1s = A("gs1s", [128, 8, 16])
    gs2s = A("gs2s", [128, 8, 16])
    b_gs = Buf("gs")
    h_state = A("h_state", [128, 8])
    halo_a = A("halo_a", [128, 8, 3])
    halo_u = A("halo_u", [128, 8, 2])
    b_hst = bufs(8, "hst")
    SCR = A.off

    M = Arena(nc, SCR, SB_END)
    NSTG = 4
    stg = [M("stg%d" % i, [128, 1024]) for i in range(NSTG)]
    b_stg = bufs(NSTG, "stg")
    hnblk = M("hnblk", [128, NCH, 512], BF16)
    b_hn = bufs(8, "hn")
    NWB = 12
    modT = nc.alloc_sbuf_tensor_at("modTall", [128, L, 48, 17], F32, offset=M.off)
    wbf = [M("wbf%d" % i, [128, 8, 128], BF16) for i in range(NWB)]
    b_wbf = bufs(NWB, "wbf")
    woutb = M("woutb", [128, 8, 1024], BF16)
    b_wout = bufs(8, "wout")
    wbda = M("wbda", [128, 8, 128])
    wbdx = M("wbdx", [128, 8, 128])
    b_wbd = Buf("wbd")
    merged = M("merged", [128, NCH, 512], BF16)
    b_mg = bufs(8, "mg")
    stateT = M("stateT", [128, 8, 96])
    b_stT = Buf("stT")
    outT = M("outT", [128, 8, 102])
    b_outT = bufs(8, "outT")
    sqr = [M("sq%d" % i, [128, 512]) for i in range(2)]
    b_sq = bufs(2, "sq")
    rstd = M("rstd", [128, 512])
    b_rstd = Buf("rstd")
    rtmp = M("rtmp", [128, 512])
    b_rtmp = Buf("rtmp")
    xa_ext2 = [M("xa_ext%d" % i, [128, 528]) for i in range(2)]
    xc2 = [M("xc%d" % i, [128, 512]) for i in range(2)]
    t_r = M("t_r", [128, 512])
    t_i = M("t_i", [128, 512])
    t_m = M("t_m", [128, 512])
    t_h = M("t_h", [128, 512])
    t_xs = M("t_xs", [128, 512])
    u_ext = M("u_ext", [128, 528])
    uc2 = [M("t_uc%d" % i, [128, 512]) for i in range(2)]
    ga2 = [M("t_ga%d" % i, [128, 512]) for i in range(2)]
    gb2 = [M("t_gb%d" % i, [128, 512]) for i in range(2)]
    b_t = {k: Buf(k) for k in ["r", "i", "m", "h", "xs", "u"]}
    b_t2 = [{k: Buf(k + str(i)) for k in ["xa", "xc", "uc", "ga", "gb"]} for i in range(2)]
    mix_end = M.off

    P = Arena(nc, SCR, SB_END)
    if dense:
        hn2all = P("hn2all", [128, NCH, NT], BF16)
        b_hn2all = bufs(NTILE, "hn2all")
        DSCR = P.off
    NG = 1 if dense else 5
    ring = [P("ring%d" % i, [128, 1024]) for i in range(NG)]
    b_ring = bufs(NG, "ring")
    wqb_off = P.off
    wqb = P("wqb", [128, 8, 2048], BF16)
    b_wq = bufs(8, "wq")
    skT = P("skT", [128, 16, 128])
    b_sk = Buf("sk")
    if dense:
        hn2T = nc.alloc_sbuf_tensor_at("hn2Tfin", [128, 8, 128], F32, offset=wqb_off)
        hn2Tb = None
    else:
        hn2T = P("hn2T", [128, 8, 128])
        hn2Tb = P("hn2Tb", [128, 8, 128], BF16)
    b_hn2 = Buf("hn2")
    b_hn2b = Buf("hn2b")
    hn2toks = [None, None] if dense else [P("hn2tok%d" % i, [128, 1024], BF16) for i in range(2)]
    b_toks = bufs(2, "tok")
    qT = P("qT", [128, 8, 128])
    b_qT = Buf("qT")
    s_sb = P("s_sb", [128, 16, 128])
    b_s = bufs(16, "s")
    work = P("work", [128, 8, 128])
    b_work = bufs(8, "work")
    cand = P("cand", [128, 4, 16, 16])
    b_cand = bufs(4, "cand")
    NGA = NG if dense else 12
    for i in range(NGA - NG):
        ring.append(P("ringx%d" % i, [128, 1024]))
        b_ring.append(Buf("ringx%d" % i))
    ringb = [r[:].bitcast(BF16) for r in ring]
    NDG = 3
    dg = [None] * NDG if dense else [P("dg%d" % i, [128, 128], BF16) for i in range(NDG)]
    b_dg = bufs(NDG, "dg")
    b_actc = bufs(16, "actc")
    b_ps = bufs(16, "prodslot")
    b_glc = bufs(16, "glc")
    b_wgc = bufs(16, "wgc")
    sv = P("sv", [128, 16, 16])
    b_sv = bufs(16, "sv")
    si = P("si", [128, 16, 16], U32)
    b_si = bufs(16, "si")
    sif = P("sif", [128, 16, 16])
    b_sif = Buf("sif")
    fv = P("fv", [128, 8, 16])
    b_fv = bufs(8, "fv")
    fi = P("fi", [128, 8, 16], U32)
    b_fi = bufs(8, "fi")
    fa_f = P("fa_f", [128, 8, 16])
    fb_f = P("fb_f", [128, 8, 16])
    fa_u = fa_f[:].bitcast(U32)
    fb_u = fb_f[:].bitcast(U32)
    isel = P("isel", [128, 128])
    jsel = P("jsel", [128, 128])
    ef = None if dense else P("ef", [128, 128])
    idxs = [None, None] if dense else [P("idx%d" % i, [128, 128], I32) for i in range(2)]
    b_idxs = bufs(2, "idx")
    ex = P("ex", [128, 8, 16])
    ssum = P("ssum", [128, 8])
    rsum = P("rsum", [128, 8])
    gws = [P("gw%d" % i, [128, 8, 16]) for i in range(2)]
    b_gws = bufs(2, "gw")
    b_rt = {k: Buf(k) for k in ["fa_u", "fb_u", "fa_f", "fb_f", "isel", "jsel", "ef", "ex", "ssum", "rsum"]}
    actv = None if dense else P("actv", [128, 128])
    gl = actv
    b_actv = Buf("actv")
    wgt = None if dense else P("wgt", [128, 128])
    b_wgt = Buf("wgt")
    b_prod = Buf("prod")
    acc = hn2T[:].rearrange("p c n -> p (c n)")
    b_acc = b_hn2
    psq = [P("psq%d" % i, [128, 128]) for i in range(2)]
    b_psq = bufs(2, "psq")
    prstd = P("prstd", [128, 128])
    b_prstd = Buf("prstd")
    prtmp = P("prtmp", [128, 128])
    b_prtmp = Buf("prtmp")
    ptmp3 = hn2T
    b_ptmp3 = b_hn2
    if dense:
        tp = P("tp", [128, 3, 128])
        b_tp = Buf("tp")
        iota128 = P("iota128", [128, 128])
        b_iota128 = Buf("iota128")
        cj = [P("cj%d" % i, [128, 128], BF16) for i in range(4)]
        b_cj = bufs(4, "cj")
        OJq = P("OJq", [128, 8, 128], BF16)
        OIq = P("OIq", [128, 8, 128], BF16)
        b_ojq = Buf("ojq")
        b_oiq = Buf("oiq")
        WTq = [P("WTq%d" % i, [128, 128, 32], BF16) for i in range(2)]
        b_wtq = bufs(2, "wtq")
        Q = Arena(nc, DSCR, SB_END)
        tabU = [Q("tabU%d" % i, [128, 8, 1024], BF16) for i in range(2)]
        tabV = [Q("tabV%d" % i, [128, 8, 1024], BF16) for i in range(2)]
        b_tu = bufs(2, "tu")
        b_tv = bufs(2, "tv")
        wtb = [Q("wtb%d" % i, [128, 8, 8, 32], BF16) for i in range(2)]
        b_wtb = bufs(2, "wtb")
        gl_sb = [Q("glsb%d" % i, [128, 256]) for i in range(2)]
        b_gl = bufs(2, "glsb")
        WA = [Q("WA%d" % i, [128, 256], BF16) for i in range(4)]
        b_wa = bufs(4, "wa")
    peer_end = P.off

    rec = [None]

    def op(*a_, **k_):
        if rec[0] is not None:
            rec[0].append(("op", a_, k_))
            return None
        return S.op(*a_, **k_)

    def dma_(*a_, **k_):
        if rec[0] is not None:
            rec[0].append(("dma", a_, k_))
            return None
        return S.dma(*a_, **k_)

    def ld(q, key, dst, src, wb, rb=()):
        return S.dma(q, key, list(rb), list(wb), out=dst, in_=src)

    ld("sp", "c0", ident[:], ident_d[:, :], [b_ident])
    ld("sp", "c0", iota[:], iota_d[:, :], [b_iota])
    op("pool", "memset", [], [b_ones], ones[:], 1.0)
    stg_ctr = [0]

    def next_stg():
        i = stg_ctr[0] % NSTG
        stg_ctr[0] += 1
        return i

    for t in range(NTILE):
        sl = next_stg()
        ld("sp", "stg%d" % sl, stg[sl][:], xin[t * 128:(t + 1) * 128, :], [b_stg[sl]])
        pb = (t % 4) * 2
        for c in range(8):
            op("pe", "transpose", [b_stg[sl], b_ident], [BK[pb + c // 4]],
               out=PS[:, pb * 512 + c * 128: pb * 512 + (c + 1) * 128],
               in_=stg[sl][:, c * 128:(c + 1) * 128], identity=ident[:])
        eng = "act" if t % 2 == 0 else "dve"
        src = PS[:, pb * 512: pb * 512 + 1024].rearrange("p (c n) -> p c n", c=8)
        if eng == "act":
            op("act", "copy", [BK[pb], BK[pb + 1]], [b_x[t]], out=xT[:, :, t * 128:(t + 1) * 128], in_=src)
        else:
            op("dve", "tensor_copy", [BK[pb], BK[pb + 1]], [b_x[t]], out=xT[:, :, t * 128:(t + 1) * 128], in_=src)

    for l in range(L):
        sl = next_stg()
        ld("sp", "stg%d" % sl, stg[sl][:, 0:128], pA_d[l, :, :], [b_stg[sl]])
        ld("sp", "stg%d" % sl, stg[sl][0:32, 128:256], pB_d[l, :, :], [b_stg[sl]])
        op("pe", "transpose", [b_stg[sl], b_ident], [BK[0]], out=PS[:, 0:128], in_=stg[sl][:, 0:128], identity=ident[:])
        op("pe", "transpose", [b_stg[sl], b_ident], [BK[0]], out=PS[:, 128:160], in_=stg[sl][0:32, 128:256],
           identity=ident[0:32, 0:32])
        op("act", "copy", [BK[0]], [b_pt], out=PT_A[:, l, :], in_=PS[:, 0:128])
        op("act", "copy", [BK[0]], [b_pt], out=PT_B[:, l, :], in_=PS[:, 128:160])
    op("act", "activation", [b_pt], [b_nsp], out=negsp8[:], in_=PT_A[:, :, 120:128], func=AF.Exp, scale=-1.0)
    op("act", "activation", [b_nsp], [b_nsp], out=negsp8[:], in_=negsp8[:], func=AF.Ln, bias=1.0, scale=1.0)
    op("dve", "tensor_scalar", [b_nsp], [b_nsp], out=negsp8[:], in0=negsp8[:], scalar1=-8.0, scalar2=None, op0=ALU.mult)

    sl = next_stg()
    ld("sp", "stg%d" % sl, stg[sl][0:17, :], cin[:, :], [b_stg[sl]])
    for c in range(8):
        op("pe", "transpose", [b_stg[sl], b_ident], [BK[1]], out=PS[:, 512 + c * 17: 512 + (c + 1) * 17],
           in_=stg[sl][0:17, c * 128:(c + 1) * 128], identity=ident[0:17, 0:17])
    op("act", "copy", [BK[1]], [b_cT], out=cT[:], in_=PS[:, 512:512 + 136].rearrange("p (c n) -> p c n", c=8))
    modtok = rtmp
    it = 0
    for l in range(L):
        for cb in range(12):
            pbk = 2 + (it % 2)
            tbk = 4 + (it % 2)
            it += 1
            for k in range(8):
                sl = next_stg()
                ld("sp", "stg%d" % sl, stg[sl][:, 0:512],
                   w_ada[l, k * 128:(k + 1) * 128, cb * 512:(cb + 1) * 512], [b_stg[sl]])
                rhs = stg[sl][:, 0:512]
                op("pe", "matmul", [b_cT, b_stg[sl]], [BK[pbk]], bank(pbk, 512, 17), lhsT=cT[:, k, :], rhs=rhs,
                   start=(k == 0), stop=(k == 7))
            op("act", "copy", [BK[pbk]], [b_rtmp], out=modtok[0:17, :], in_=bank(pbk, 512, 17))
            for j in range(4):
                op("pe", "transpose", [b_rtmp, b_ident], [BK[tbk]], out=PS[:, tbk * 512 + j * 17: tbk * 512 + (j + 1) * 17],
                   in_=modtok[0:17, j * 128:(j + 1) * 128], identity=ident[0:17, 0:17])
            op("dve", "tensor_tensor", [BK[tbk], b_pt], [b_modall], out=modT[:, l, cb * 4:(cb + 1) * 4, :],
               in0=PS[:, tbk * 512: tbk * 512 + 68].rearrange("p (j n) -> p j n", j=4),
               in1=PT_A[:, l, cb * 4:(cb + 1) * 4].unsqueeze(2).broadcast_to([128, 4, 17]), op=ALU.add)

    S.dma("sp", "modD", [b_modall], [b_modD], out=modD[:, :], in_=modT[:].rearrange("p l c n -> p (l c n)"))
    S.barrier()

    blocks = [(0, 512), (512, 512), (1024, 512), (1536, 512), (2048, 128)]

    def xbufs(s0, n):
        return b_x[s0 // 128:(s0 + n) // 128]

    def v3(ap, inner):
        return ap.rearrange("p (s t) -> p s t", t=inner)

    def bc_seq(ap16, inner):
        return ap16.unsqueeze(2).broadcast_to([128, 16, inner])

    def rms_rstd(src3, n, rb, sq_t, b_sq_t, rstd_t, b_rstd_t, rtmp_t, b_rtmp_t, pbk):
        for c in range(8):
            q = c % 2
            op("act", "activation", rb, [b_sq_t[q]], out=sq_t[q][:, :n], in_=src3[:, c, :], func=AF.Square)
            op("pe", "matmul", [b_sq_t[q], b_ones], [BK[pbk]], bank(pbk, n), lhsT=ones[:], rhs=sq_t[q][:, :n],
               start=(c == 0), stop=(c == 7))
        op("act", "activation", [BK[pbk]], [b_rtmp_t], out=rtmp_t[:, :n], in_=bank(pbk, n), func=AF.Sqrt,
           bias=EPS, scale=1.0 / D)
        op("dve", "reciprocal", [b_rtmp_t], [b_rstd_t], out=rstd_t[:, :n], in_=rtmp_t[:, :n])

    for l in range(n_layers):
        for k in range(8):
            S.dma("pool", "wout", [], [b_wout[k]], out=woutb[:, k, :], in_=w_out[l, k * 128:(k + 1) * 128, :])
        if do_peer and not dense:
            for ch in range(16):
                r0 = ch * 1024
                S.dma("pool", "cv", [], [b_uv[l]], out=uvb_d[l][r0:r0 + 1024, 0:D], in_=u_d[l][r0:r0 + 1024, :])
                S.dma("pool", "cv", [], [b_uv[l]], out=uvb_d[l][r0:r0 + 1024, D:2 * D], in_=v_d[l][r0:r0 + 1024, :])
        ld("sp", "wbd", wbda[:], wbda_d[l].rearrange("c p m -> p c m"), [b_wbd])
        ld("sp", "wbd", wbdx[:], wbdx_d[l].rearrange("c p m -> p c m"), [b_wbd])
        sl = next_stg()
        ld("sp", "stg%d" % sl, stg[sl][0:96, :], state_d[l, :, :], [b_stg[sl]])
        for c in range(8):
            op("pe", "transpose", [b_stg[sl], b_ident], [BK[c // 4]], out=PS[:, c * 128:c * 128 + 96],
               in_=stg[sl][0:96, c * 128:(c + 1) * 128], identity=ident[0:96, 0:96])
        op("act", "copy", [BK[0], BK[1]], [b_stT], out=stateT[:], in_=PS[:, 0:1024].rearrange("p (c n) -> p c n", c=8)[:, :, 0:96])
        ld("sp", "modL", modL[:].rearrange("p c n -> p (c n)"), modD[:, l * 816:(l + 1) * 816], [b_mod], rb=[b_modD])
        op("dve", "scalar_tensor_tensor", [b_mod, b_pt], [b_gs], out=gs1p[:], in0=modL[:, 8:16, 0], scalar=1.0,
           in1=PT_A[:, l, 48:56], op0=ALU.add, op1=ALU.mult)
        op("dve", "scalar_tensor_tensor", [b_mod, b_pt], [b_gs], out=gs2p[:], in0=modL[:, 32:40, 0], scalar=1.0,
           in1=PT_A[:, l, 56:64], op0=ALU.add, op1=ALU.mult)
        op("dve", "scalar_tensor_tensor", [b_mod, b_pt], [b_gs], out=gs1s[:], in0=modL[:, 8:16, 1:17], scalar=1.0,
           in1=PT_A[:, l, 48:56].unsqueeze(2).broadcast_to([128, 8, 16]), op0=ALU.add, op1=ALU.mult)
        op("dve", "scalar_tensor_tensor", [b_mod, b_pt], [b_gs], out=gs2s[:], in0=modL[:, 32:40, 1:17], scalar=1.0,
           in1=PT_A[:, l, 56:64].unsqueeze(2).broadcast_to([128, 8, 16]), op0=ALU.add, op1=ALU.mult)

        iters = [(tb, c) for tb in range(5) for c in range(8)]

        def load_w(n):
            tb, c = iters[n]
            for g in range(6):
                col0 = g * 1024 + c * 128
                wi = (n % 2) * 6 + g
                S.dma("pool", "wbf%d" % wi, [], [b_wbf[wi]], out=wbf[wi][:],
                      in_=w_in[l][:, col0:col0 + 128].rearrange("(k p) m -> p k m", p=128))

        load_w(0)

        def part1(n):
            tb, c = iters[n]
            s0, N = blocks[tb]
            samp = (tb == 4)
            xb = xbufs(s0, N)
            p = n % 2
            xa_ext, xc, t_uc, t_ga, t_gb, bp = xa_ext2[p], xc2[p], uc2[p], ga2[p], gb2[p], b_t2[p]
            if c == 0:
                rms_rstd(xT[:, :, s0:s0 + N], N, xb, sqr, b_sq, rstd, b_rstd, rtmp, b_rtmp, 6)
                for cc in range(8):
                    q = cc % 2
                    if not samp:
                        op("dve", "scalar_tensor_tensor", xb + [b_gs, b_rstd], [b_sq[q]], out=sqr[q][:, :N],
                           in0=xT[:, cc, s0:s0 + N], scalar=gs1p[:, cc:cc + 1], in1=rstd[:, :N], op0=ALU.mult, op1=ALU.mult)
                        op("act", "activation", [b_sq[q], b_mod], [b_hn[cc]], out=hnblk[:, cc, :N], in_=sqr[q][:, :N],
                           func=AF.Identity, bias=modL[:, cc, 0:1], scale=1.0)
                    else:
                        op("dve", "tensor_tensor", xb + [b_rstd], [b_sq[q]], out=sqr[q][:, :N], in0=xT[:, cc, s0:s0 + N],
                           in1=rstd[:, :N], op=ALU.mult)
                        op("dve", "tensor_tensor", [b_sq[q], b_gs], [b_sq[q]], out=v3(sqr[q][:, :N], 8), in0=v3(sqr[q][:, :N], 8),
                           in1=bc_seq(gs1s[:, cc, :], 8), op=ALU.mult)
                        op("dve", "tensor_tensor", [b_sq[q], b_mod], [b_hn[cc]], out=v3(hnblk[:, cc, :N], 8),
                           in0=v3(sqr[q][:, :N], 8), in1=bc_seq(modL[:, cc, 1:17], 8), op=ALU.add)
            if n + 1 < len(iters):
                load_w(n + 1)
            ws = (n % 2) * 6

            def proj(g):
                for k in range(8):
                    op("pe", "matmul", [b_wbf[ws + g], b_hn[k]], [BK[g]], bank(g, N), lhsT=wbf[ws + g][:, k, :],
                       rhs=hnblk[:, k, :N], start=(k == 0), stop=(k == 7))
            proj(0)
            if not samp:
                XA = xa_ext[:, 3:3 + N]
                if tb == 0:
                    op("pool", "memset", [], [bp["xa"]], xa_ext[:, 0:3], 0.0)
                else:
                    op("pool", "tensor_copy", [b_hst[c]], [bp["xa"]], out=xa_ext[:, 0:3], in_=halo_a[:, c, :])
                op("act", "copy", [BK[0]], [bp["xa"]], out=XA, in_=bank(0, N))
                if tb < 3:
                    op("pool", "tensor_copy", [bp["xa"]], [b_hst[c]], out=halo_a[:, c, :], in_=xa_ext[:, N:N + 3])
                else:
                    op("pool", "tensor_copy", [bp["xa"]], [b_outT[c]], out=outT[:, c, 96:99], in_=xa_ext[:, N:N + 3])

                def xsh(k):
                    return xa_ext[:, k:k + N]
                XC = xc[:, :N]
            else:
                xa3 = xa_ext[:, 0:176].rearrange("p (s t) -> p s t", t=11)
                op("pool", "tensor_copy", [b_stT], [bp["xa"]], out=xa3[:, :, 0:3],
                   in_=stateT[:, c, 0:48].rearrange("p (s k) -> p s k", k=3))
                op("act", "copy", [BK[0]], [bp["xa"]], out=xa3[:, :, 3:11], in_=v3(bank(0, N), 8))
                op("pool", "tensor_copy", [bp["xa"]], [b_outT[c]], out=outT[:, c, 0:48].rearrange("p (s k) -> p s k", k=3),
                   in_=xa3[:, :, 8:11])

                def xsh(k):
                    return xa3[:, :, k:k + 8]
                XC = v3(xc[:, :N], 8)
            op("act", "activation", [bp["xa"], b_pt], [bp["xc"]], out=XC, in_=xsh(0), func=AF.Identity,
               scale=PT_A[:, l, 64 + c:65 + c], bias=PT_A[:, l, 96 + c:97 + c])
            for k in range(1, 4):
                op("dve", "scalar_tensor_tensor", [bp["xa"], b_pt, bp["xc"]], [bp["xc"]], out=XC, in0=xsh(k),
                   scalar=PT_A[:, l, 64 + k * 8 + c:65 + k * 8 + c], in1=XC, op0=ALU.mult, op1=ALU.add)
            proj(3)
            op("act", "copy", [BK[3]], [b_t["xs"]], out=t_xs[:, :N], in_=bank(3, N))
            proj(2)
            if not samp:
                if tb == 0:
                    op("pool", "memset", [], [b_t["u"]], u_ext[:, 0:2], 0.0)
                else:
                    op("pool", "tensor_copy", [b_hst[c]], [b_t["u"]], out=u_ext[:, 0:2], in_=halo_u[:, c, :])
                op("dve", "tensor_tensor", [BK[2], b_t["xs"]], [b_t["u"]], out=u_ext[:, 2:2 + N], in0=bank(2, N), in1=t_xs[:, :N],
                   op=ALU.mult)
                if tb < 3:
                    op("pool", "tensor_copy", [b_t["u"]], [b_hst[c]], out=halo_u[:, c, :], in_=u_ext[:, N:N + 2])
                else:
                    op("pool", "tensor_copy", [b_t["u"]], [b_outT[c]], out=outT[:, c, 100:102], in_=u_ext[:, N:N + 2])

                def ush(k):
                    return u_ext[:, k:k + N]
                UC = t_uc[:, :N]
                BB = bank(1, N)
            else:
                u3 = u_ext[:, 0:160].rearrange("p (s t) -> p s t", t=10)
                op("pool", "tensor_copy", [b_stT], [b_t["u"]], out=u3[:, :, 0:2],
                   in_=stateT[:, c, 64:96].rearrange("p (s k) -> p s k", k=2))
                op("dve", "tensor_tensor", [BK[2], b_t["xs"]], [b_t["u"]], out=u3[:, :, 2:10], in0=v3(bank(2, N), 8),
                   in1=v3(t_xs[:, :N], 8), op=ALU.mult)
                op("pool", "tensor_copy", [b_t["u"]], [b_outT[c]], out=outT[:, c, 64:96].rearrange("p (s k) -> p s k", k=2),
                   in_=u3[:, :, 8:10])

                def ush(k):
                    return u3[:, :, k:k + 8]
                UC = v3(t_uc[:, :N], 8)
                BB = v3(bank(1, N), 8)
            op("act", "activation", [b_t["u"], b_pt], [bp["uc"]], out=UC, in_=ush(0), func=AF.Copy, scale=PT_B[:, l, c:c + 1])
            for k in range(1, 3):
                op("dve", "scalar_tensor_tensor", [b_t["u"], b_pt, bp["uc"]], [bp["uc"]], out=UC, in0=ush(k),
                   scalar=PT_B[:, l, k * 8 + c:k * 8 + c + 1], in1=UC, op0=ALU.mult, op1=ALU.add)
            proj(1)
            op("dve", "tensor_tensor", [BK[1], bp["uc"]], [bp["uc"]], out=UC, in0=BB, in1=UC, op=ALU.mult)
            proj(4)
            op("act", "activation", [BK[4]], [bp["ga"]], out=t_ga[:, :N], in_=bank(4, N), func=AF.Sigmoid)
            proj(5)
            op("act", "activation", [BK[5]], [bp["gb"]], out=t_gb[:, :N], in_=bank(5, N), func=AF.Sigmoid)

        def part2(n):
            tb, c = iters[n]
            s0, N = blocks[tb]
            samp = (tb == 4)
            xb = xbufs(s0, N)
            p = n % 2
            xc, t_uc, t_ga, t_gb, bp = xc2[p], uc2[p], ga2[p], gb2[p], b_t2[p]
            op("pe", "matmul", [b_wbd, bp["xc"]], [BK[6]], bank(6, N), lhsT=wbda[:, c, :], rhs=xc[:, :N], start=True, stop=True)
            op("pe", "matmul", [b_wbd, bp["xc"]], [BK[7]], bank(7, N), lhsT=wbdx[:, c, :], rhs=xc[:, :N], start=True, stop=True)
            op("act", "activation", [BK[6], b_pt], [b_t["r"]], out=t_r[:, :N], in_=bank(6, N), func=AF.Sigmoid,
               bias=PT_A[:, l, 104 + c:105 + c], scale=1.0)
            op("act", "activation", [BK[7], b_pt], [b_t["i"]], out=t_i[:, :N], in_=bank(7, N), func=AF.Sigmoid,
               bias=PT_A[:, l, 112 + c:113 + c], scale=1.0)
            op("act", "activation", [b_t["r"], b_nsp], [b_t["r"]], out=t_r[:, :N], in_=t_r[:, :N], func=AF.Exp,
               scale=negsp8[:, l, c:c + 1])
            op("act", "activation", [b_t["r"]], [b_t["m"]], out=t_m[:, :N], in_=t_r[:, :N], func=AF.Square)
            op("act", "activation", [b_t["m"]], [b_t["m"]], out=t_m[:, :N], in_=t_m[:, :N], func=AF.Sqrt, bias=1.0, scale=-1.0)
            if tb == 0:
                op("pool", "memset", [b_t["m"]], [b_t["m"]], t_m[:, 0:1], 1.0)
            op("dve", "tensor_tensor", [b_t["i"], bp["xc"]], [b_t["i"]], out=t_i[:, :N], in0=t_i[:, :N], in1=xc[:, :N], op=ALU.mult)
            op("dve", "tensor_tensor", [b_t["i"], b_t["m"]], [b_t["i"]], out=t_i[:, :N], in0=t_i[:, :N], in1=t_m[:, :N], op=ALU.mult)
            if not samp:
                if tb == 0:
                    init = 0.0
                    rdi = []
                else:
                    init = h_state[:, c:c + 1]
                    rdi = [b_hst[c]]
                op("dve", "tensor_tensor_scan", [b_t["r"], b_t["i"]] + rdi, [b_t["h"]], out=t_h[:, :N], data0=t_r[:, :N],
                   data1=t_i[:, :N], initial=init, op0=ALU.mult, op1=ALU.add)
                if tb < 3:
                    op("pool", "tensor_copy", [b_t["h"]], [b_hst[c]], out=h_state[:, c:c + 1], in_=t_h[:, N - 1:N])
                else:
                    op("pool", "tensor_copy", [b_t["h"]], [b_outT[c]], out=outT[:, c, 99:100], in_=t_h[:, N - 1:N])
            else:
                a3 = v3(t_r[:, :N], 8)
                b3 = v3(t_i[:, :N], 8)
                op("dve", "tensor_tensor", [b_t["r"], b_stT], [b_t["m"]], out=t_m[:, 0:16], in0=a3[:, :, 0],
                   in1=stateT[:, c, 48:64], op=ALU.mult)
                op("dve", "tensor_tensor", [b_t["i"], b_t["m"]], [b_t["i"]], out=b3[:, :, 0], in0=b3[:, :, 0], in1=t_m[:, 0:16],
                   op=ALU.add)
                op("dve", "memset", [], [b_t["r"]], a3[:, :, 0], 0.0)
                op("dve", "tensor_tensor_scan", [b_t["r"], b_t["i"]], [b_t["h"]], out=t_h[:, :N], data0=t_r[:, :N],
                   data1=t_i[:, :N], initial=0.0, op0=ALU.mult, op1=ALU.add)
                op("pool", "tensor_copy", [b_t["h"]], [b_outT[c]], out=outT[:, c, 48:64], in_=v3(t_h[:, :N], 8)[:, :, 7])
            op("dve", "tensor_tensor", [bp["ga"], b_t["h"]], [bp["ga"]], out=t_ga[:, :N], in0=t_ga[:, :N], in1=t_h[:, :N], op=ALU.mult)
            op("dve", "tensor_tensor", [bp["gb"], bp["uc"]], [bp["gb"]], out=t_gb[:, :N], in0=t_gb[:, :N], in1=t_uc[:, :N],
               op=ALU.mult)
            op("dve", "tensor_tensor", [bp["ga"], bp["gb"]], [b_mg[c]], out=merged[:, c, :N], in0=t_ga[:, :N], in1=t_gb[:, :N],
               op=ALU.add)
            if c == 7:
                for oc in range(8):
                    pbk = 6 + (oc % 2)
                    for k in range(8):
                        op("pe", "matmul", [b_wout[k], b_mg[k]], [BK[pbk]], bank(pbk, N), lhsT=woutb[:, k, oc * 128:(oc + 1) * 128],
                           rhs=merged[:, k, :N], start=(k == 0), stop=(k == 7))
                    if not samp:
                        op("dve", "scalar_tensor_tensor", [BK[pbk], b_mod] + xb, xb, out=xT[:, oc, s0:s0 + N], in0=bank(pbk, N),
                           scalar=modL[:, 16 + oc, 0:1], in1=xT[:, oc, s0:s0 + N], op0=ALU.mult, op1=ALU.add)
                    else:
                        op("dve", "tensor_tensor", [BK[pbk], b_mod], [b_rtmp], out=v3(rtmp[:, :N], 8), in0=v3(bank(pbk, N), 8),
                           in1=bc_seq(modL[:, 16 + oc, 1:17], 8), op=ALU.mult)
                        op("dve", "tensor_tensor", [b_rtmp] + xb, xb, out=xT[:, oc, s0:s0 + N], in0=rtmp[:, :N],
                           in1=xT[:, oc, s0:s0 + N], op=ALU.add)

        part1(0)
        for n in range(len(iters)):
            if n + 1 < len(iters) and iters[n][1] != 7:
                part1(n + 1)
                part2(n)
            else:
                part2(n)
                if n + 1 < len(iters):
                    part1(n + 1)
        for c in range(8):
            op("pe", "transpose", [b_outT[c], b_ident], [BK[c // 4]], out=PS[0:102, c * 128:(c + 1) * 128], in_=outT[:, c, :],
               identity=ident[:])
        sl = next_stg()
        op("act", "copy", [BK[0], BK[1]], [b_stg[sl]], out=stg[sl][0:102, :], in_=PS[0:102, 0:1024])
        ob = Buf("ost%d" % l)
        O_bufs.append(ob)
        S.dma("sp", "ost", [b_stg[sl]], [ob], out=ost_d[l, :, :], in_=stg[sl][0:102, :])

        if not do_peer:
            continue
        S.barrier()
        rg = [0]

        def next_ring():
            i = rg[0] % NG
            rg[0] += 1
            return i

        for k in range(8):
            S.dma("pool", "wq", [], [b_wq[k]], out=wqb[:, k, :], in_=w_q[l, k * 128:(k + 1) * 128, :])
        ld("sp", "sk", skT[:], skT_d[l].rearrange("h p k -> p h k"), [b_sk])
        def phaseA1(t):
            s0 = t * 128
            samp = (t == 16)
            xb = [b_x[t]]
            hn2tok = hn2toks[t % 2]
            b_tok = b_toks[t % 2]
            idx = idxs[t % 2]
            b_idx = b_idxs[t % 2]
            gw = gws[t % 2]
            b_gw = b_gws[t % 2]
            ab = 2 if t % 2 == 0 else 6
            rms_rstd(xT[:, :, s0:s0 + 128], 128, xb, psq, b_psq, prstd, b_prstd, prtmp, b_prtmp, 0)
            hb_ = b_hn2all[t] if dense else b_hn2

            def hdst(cc_):
                return hn2all[:, cc_, s0:s0 + 128] if dense else hn2T[:, cc_, :]
            for cc in range(8):
                q = cc % 2
                if not samp:
                    op("dve", "scalar_tensor_tensor", xb + [b_gs, b_prstd], [b_psq[q]], out=psq[q][:], in0=xT[:, cc, s0:s0 + 128],
                       scalar=gs2p[:, cc:cc + 1], in1=prstd[:], op0=ALU.mult, op1=ALU.mult)
                    op("act", "activation", [b_psq[q], b_mod], [hb_], out=hdst(cc), in_=psq[q][:], func=AF.Identity,
                       bias=modL[:, 24 + cc, 0:1], scale=1.0)
                else:
                    op("dve", "tensor_tensor", xb + [b_prstd], [b_psq[q]], out=psq[q][:], in0=xT[:, cc, s0:s0 + 128], in1=prstd[:],
                       op=ALU.mult)
                    op("dve", "tensor_tensor", [b_psq[q], b_gs], [b_psq[q]], out=v3(psq[q][:], 8), in0=v3(psq[q][:], 8),
                       in1=bc_seq(gs2s[:, cc, :], 8), op=ALU.mult)
                    op("dve", "tensor_tensor", [b_psq[q], b_mod], [hb_], out=v3(hdst(cc), 8), in0=v3(psq[q][:], 8),
                       in1=bc_seq(modL[:, 24 + cc, 1:17], 8), op=ALU.add)
            if dense:
                qrhs = hn2all[:, :, s0:s0 + 128]
                qrb = b_hn2all[t]
            else:
                op("act", "copy", [b_hn2], [b_hn2b], out=hn2Tb[:], in_=hn2T[:])
                qrhs = hn2Tb
                qrb = b_hn2b
                for cc in range(8):
                    op("pe", "transpose", [b_hn2, b_ident], [BK[cc // 4]], out=PS[:, cc * 128:(cc + 1) * 128],
                       in_=hn2T[:, cc, :], identity=ident[:])
                op("act", "copy", [BK[0], BK[1]], [b_tok], out=hn2tok[:], in_=PS[:, 0:1024])
            for g8 in range(2):
                for hp in range(g8 * 8, g8 * 8 + 8):
                    h8 = hp % 8
                    for k in range(8):
                        op("pe", "matmul", [b_wq[k], qrb], [BK[4 + h8 // 4]], PS[:, 2048 + h8 * 128: 2048 + (h8 + 1) * 128],
                           lhsT=wqb[:, k, hp * 128:(hp + 1) * 128], rhs=qrhs[:, k, :], start=(k == 0), stop=(k == 7))
                op("act", "copy", BK[4:6], [b_qT], out=qT[:].rearrange("p h n -> p (h n)"), in_=PS[:, 2048:3072])
                for hp in range(g8 * 8, g8 * 8 + 8):
                    h8 = hp % 8
                    op("pe", "matmul", [b_qT, b_sk], [BK[4 + h8 // 4]], PS[:, 2048 + h8 * 128: 2048 + (h8 + 1) * 128], lhsT=qT[:, h8, :],
                       rhs=skT[:, hp, :], start=True, stop=True)
                op("act", "copy", BK[4:6], b_s[g8 * 8:(g8 + 1) * 8], out=s_sb[:, g8 * 8:(g8 + 1) * 8, :].rearrange("p h n -> p (h n)"),
                   in_=PS[:, 2048:3072])
        def phaseA2(t):
            idx = idxs[t % 2]
            b_idx = b_idxs[t % 2]
            gw = gws[t % 2]
            b_gw = b_gws[t % 2]
            ab = 2 if t % 2 == 0 else 6
            ops = []

            def q(e, meth, R, W, *a_, **k_):
                ops.append(("op", (e, meth, list(R), list(W)) + a_, k_))
            for g8 in range(2):
                hps = range(g8 * 8, g8 * 8 + 8)
                for hp in hps:
                    q("dve", "max", [b_s[hp]], [b_sv[hp]], out=sv[:, hp, 0:8], in_=s_sb[:, hp, :])
                for hp in hps:
                    q("dve", "max_index", [b_s[hp], b_sv[hp]], [b_si[hp]], out=si[:, hp, 0:8], in_max=sv[:, hp, 0:8],
                      in_values=s_sb[:, hp, :])
                for hp in hps:
                    q("dve", "match_replace", [b_s[hp], b_sv[hp]], [b_work[hp % 8]], out=work[:, hp % 8, :],
                      in_to_replace=sv[:, hp, 0:8], in_values=s_sb[:, hp, :], imm_value=-1e30)
                for hp in hps:
                    q("dve", "max", [b_work[hp % 8]], [b_sv[hp]], out=sv[:, hp, 8:16], in_=work[:, hp % 8, :])
                for hp in hps:
                    q("dve", "max_index", [b_work[hp % 8], b_sv[hp]], [b_si[hp]], out=si[:, hp, 8:16], in_max=sv[:, hp, 8:16],
                      in_values=work[:, hp % 8, :])
            q("dve", "tensor_copy", b_si, [b_sif], out=sif[:], in_=si[:])
            svv = sv[:].rearrange("p (h t) k -> p h t k", t=2)
            sfv = sif[:].rearrange("p (h t) k -> p h t k", t=2)
            cflat = cand[:].rearrange("p h a b -> p h (a b)")
            wflat = work[:].rearrange("p (h t) n -> p h (t n)", t=2)
            for g4 in range(2):
                hs = range(g4 * 4, g4 * 4 + 4)
                h0 = g4 * 4
                q("dve", "tensor_tensor", b_sv[2 * h0:2 * h0 + 8], b_cand, out=cand[:],
                  in0=svv[:, h0:h0 + 4, 0, :].unsqueeze(3).broadcast_to([128, 4, 16, 16]),
                  in1=svv[:, h0:h0 + 4, 1, :].unsqueeze(2).broadcast_to([128, 4, 16, 16]), op=ALU.add)
                for h in hs:
                    q("dve", "max", [b_cand[h % 4]], [b_fv[h]], out=fv[:, h, 0:8], in_=cflat[:, h % 4, :])
                for h in hs:
                    q("dve", "max_index", [b_cand[h % 4], b_fv[h]], [b_fi[h]], out=fi[:, h, 0:8], in_max=fv[:, h, 0:8],
                      in_values=cflat[:, h % 4, :])
                for h in hs:
                    q("dve", "match_replace", [b_cand[h % 4], b_fv[h]], [b_work[2 * (h % 4)], b_work[2 * (h % 4) + 1]],
                      out=wflat[:, h % 4, :], in_to_replace=fv[:, h, 0:8], in_values=cflat[:, h % 4, :], imm_value=-1e30)
                for h in hs:
                    q("dve", "max", [b_work[2 * (h % 4)], b_work[2 * (h % 4) + 1]], [b_fv[h]], out=fv[:, h, 8:16],
                      in_=wflat[:, h % 4, :])
                for h in hs:
                    q("dve", "max_index", [b_work[2 * (h % 4)], b_work[2 * (h % 4) + 1], b_fv[h]], [b_fi[h]], out=fi[:, h, 8:16],
                      in_max=fv[:, h, 8:16], in_values=wflat[:, h % 4, :])
            q("dve", "tensor_single_scalar", b_fi, [b_rt["fa_f"]], out=fa_u, in_=fi[:], scalar=4, op=ALU.logical_shift_right)
            q("dve", "tensor_single_scalar", b_fi, [b_rt["fb_f"]], out=fb_u, in_=fi[:], scalar=15, op=ALU.bitwise_and)
            q("dve", "tensor_copy", [b_rt["fa_f"]], [b_rt["fa_f"]], out=fa_f[:], in_=fa_u)
            q("dve", "tensor_copy", [b_rt["fb_f"]], [b_rt["fb_f"]], out=fb_f[:], in_=fb_u)
            iota4 = iota[:].unsqueeze(1).unsqueeze(1).broadcast_to([128, 4, 16, 16])
            for (ff_, half, dst, dk, fk) in ((fa_f, 0, isel, "isel", "fa_f"), (fb_f, 1, jsel, "jsel", "fb_f")):
                for g4 in range(2):
                    h0 = g4 * 4
                    q("dve", "tensor_tensor", [b_rt[fk], b_iota], b_cand, out=cand[:],
                      in0=ff_[:, h0:h0 + 4, :].unsqueeze(3).broadcast_to([128, 4, 16, 16]), in1=iota4, op=ALU.is_equal)
                    q("dve", "tensor_tensor", b_cand + [b_sif], b_cand, out=cand[:], in0=cand[:],
                      in1=sfv[:, h0:h0 + 4, half, :].unsqueeze(2).broadcast_to([128, 4, 16, 16]), op=ALU.mult)
                    q("dve", "tensor_reduce", b_cand, [b_rt[dk]], out=dst[:, h0 * 16:(h0 + 4) * 16],
                      in_=cand[:].rearrange("p h k a -> p (h k) a"), axis=AX.X, op=ALU.add)
            if not dense:
                q("dve", "scalar_tensor_tensor", [b_rt["isel"], b_rt["jsel"]], [b_rt["ef"]], out=ef[:], in0=isel[:], scalar=128.0,
                  in1=jsel[:], op0=ALU.mult, op1=ALU.add)
                q("dve", "tensor_copy", [b_rt["ef"]], [b_idx], out=idx[:], in_=ef[:])
            q("dve", "tensor_tensor", b_fv, [b_rt["ex"]], out=ex[:], in0=fv[:], in1=fv[:, :, 0:1].broadcast_to([128, 8, 16]),
              op=ALU.subtract)
            q("act", "activation", [b_rt["ex"]], [b_rt["ex"]], out=ex[:], in_=ex[:], func=AF.Exp)
            q("dve", "tensor_reduce", [b_rt["ex"]], [b_rt["ssum"]], out=ssum[:], in_=ex[:], axis=AX.X, op=ALU.add)
            q("dve", "reciprocal", [b_rt["ssum"]], [b_rt["rsum"]], out=rsum[:], in_=ssum[:])
            q("dve", "tensor_tensor", [b_rt["ex"], b_rt["rsum"]], [b_gw], out=gw[:], in0=ex[:],
              in1=rsum[:].unsqueeze(2).broadcast_to([128, 8, 16]), op=ALU.mult)
            return ops

        def emit(ops, n):
            for _ in range(min(n, len(ops))):
                kind_, a_, k_ = ops.pop(0)
                if kind_ == "op":
                    S.op(*a_, **k_)
                else:
                    S.dma(*a_, **k_)

        def deferred_front(t):
            rec[0] = []
            phaseA1(t)
            ops = rec[0]
            rec[0] = None
            return ops + phaseA2(t)

        def phaseB(t, pend):
            per_hk = (len(pend) + 109) // 110
            s0 = t * 128
            samp = (t == 16)
            xb = [b_x[t]]
            hn2tok = hn2toks[t % 2]
            b_tok = b_toks[t % 2]
            idx = idxs[t % 2]
            b_idx = b_idxs[t % 2]
            gw = gws[t % 2]
            b_gw = b_gws[t % 2]
            ab = 2 if t % 2 == 0 else 6
            gwf = gw[:].rearrange("p h k -> p (h k)")
            LAG = 2
            slots = []

            def tail(hk_):
                sl_, rb_ = slots[hk_]
                q4_ = hk_ % 16
                d8_ = hk_ % NDG
                op("dve", "tensor_scalar", [b_glc[q4_], b_gw, b_ident], [b_dg[d8_]], out=dg[d8_][:], in0=ident[:],
                   scalar1=gl[:, hk_:hk_ + 1], scalar2=gwf[:, hk_:hk_ + 1], op0=ALU.mult, op1=ALU.mult)
                for j in range(2):
                    op("pe", "matmul", [b_dg[d8_], b_ring[sl_]], [BK[ab + j]], bank(ab + j), lhsT=dg[d8_][:],
                       rhs=rb_[:, D + 512 * j:D + 512 * (j + 1)], start=(hk_ == 0), stop=(hk_ == 127))

            for hk in range(128):
                sl = rg[0] % NGA
                rg[0] += 1
                rb = ringb[sl]
                slots.append((sl, rb))
                q4 = hk % 16
                S.dma("pool", "ring%d" % sl, [b_idx, b_uv[l]], [b_ring[sl]], meth="indirect_dma_start", out=rb, out_offset=None,
                      in_=uvb_d[l][:, :], in_offset=bass.IndirectOffsetOnAxis(ap=idx[:, hk:hk + 1], axis=0))
                op("dve", "tensor_tensor", [b_tok], [b_ps[sl]], out=rb[:, 0:D], in0=rb[:, 0:D], in1=hn2tok[:], op=ALU.mult,
                   soft=[b_ring[sl]])
                op("act", "activation", [b_ps[sl]], [b_actc[q4]], out=rb[:, 0:D], in_=rb[:, 0:D], func=AF.Copy,
                   accum_out=actv[:, hk:hk + 1])
                op("act", "activation", [b_actc[q4]], [b_glc[q4]], out=gl[:, hk:hk + 1], in_=actv[:, hk:hk + 1], func=AF.Gelu)
                if hk >= LAG:
                    tail(hk - LAG)
                emit(pend, per_hk)
            for hk in range(128 - LAG, 128):
                tail(hk)
            emit(pend, len(pend))
        def phaseC(t):
            s0 = t * 128
            samp = (t == 16)
            xb = [b_x[t]]
            hn2tok = hn2toks[t % 2]
            b_tok = b_toks[t % 2]
            idx = idxs[t % 2]
            b_idx = b_idxs[t % 2]
            gw = gws[t % 2]
            b_gw = b_gws[t % 2]
            ab = 2 if t % 2 == 0 else 6
            op("act", "copy", [BK[ab], BK[ab + 1]], [b_acc], out=acc, in_=PS[:, ab * 512:ab * 512 + 1024])
            for cc in range(8):
                op("pe", "transpose", [b_acc, b_ident], [BK[cc // 4]], out=PS[:, cc * 128:(cc + 1) * 128],
                   in_=acc[:, cc * 128:(cc + 1) * 128], identity=ident[:])
            pv = PS[:, 0:1024].rearrange("p (c n) -> p c n", c=8)
            if not samp:
                for cc in range(8):
                    op("dve", "scalar_tensor_tensor", [BK[cc // 4], b_mod] + xb, xb, out=xT[:, cc, s0:s0 + 128], in0=pv[:, cc, :],
                       scalar=modL[:, 40 + cc, 0:1], in1=xT[:, cc, s0:s0 + 128], op0=ALU.mult, op1=ALU.add)
            else:
                for cc in range(8):
                    op("dve", "tensor_tensor", [BK[cc // 4], b_mod], [b_ptmp3], out=v3(ptmp3[:, cc, :], 8), in0=v3(pv[:, cc, :], 8),
                       in1=bc_seq(modL[:, 40 + cc, 1:17], 8), op=ALU.mult)
                op("dve", "tensor_tensor", [b_ptmp3] + xb, xb, out=xT[:, :, s0:s0 + 128], in0=ptmp3[:], in1=xT[:, :, s0:s0 + 128],
                   op=ALU.add)


        if dense:
            ld("sp", "c0", iota128[:], iota128_d[:, :], [b_iota128])
            wc = [0]
            occ = [0]

            iota3 = iota128[:].unsqueeze(1).broadcast_to([128, 8, 128])

            def phaseA3(t):
                gw = gws[t % 2]
                b_gw = b_gws[t % 2]
                srcs = ((isel[:], b_rt["isel"]), (jsel[:], b_rt["jsel"]), (gw[:].rearrange("p h k -> p (h k)"), b_gw))
                for n_, (sap, sb_) in enumerate(srcs):
                    op("pe", "transpose", [sb_, b_ident], [BK[1]], out=PS[:, 512 + n_ * 128:512 + (n_ + 1) * 128], in_=sap,
                       identity=ident[:])
                op("act", "copy", [BK[1]], [b_tp], out=tp[:].rearrange("p n t -> p (n t)"), in_=PS[:, 512:896])
                for g8 in range(16):
                    t0_ = g8 * 8
                    if g8 % 4 == 0:
                        ws_ = wc[0] % 2
                        wc[0] += 1
                    op("dve", "tensor_tensor", [b_tp, b_iota128], [b_ojq], out=OJq[:], in0=iota3,
                       in1=tp[:, 1, t0_:t0_ + 8].unsqueeze(2).broadcast_to([128, 8, 128]), op=ALU.is_equal)
                    op("dve", "tensor_tensor", [b_tp, b_iota128], [b_oiq], out=OIq[:], in0=iota3,
                       in1=tp[:, 0, t0_:t0_ + 8].unsqueeze(2).broadcast_to([128, 8, 128]), op=ALU.is_equal)
                    for tt in range(8):
                        tl = t0_ + tt
                        o = occ[0] % 4
                        occ[0] += 1
                        op("act", "activation", [b_oiq, b_tp], [b_cj[o]], out=cj[o][:], in_=OIq[:, tt, :], func=AF.Copy,
                           scale=tp[:, 2, tl:tl + 1])
                        pb = 2 + ((tl // 4) % 2)
                        op("pe", "matmul", [b_ojq, b_cj[o]], [BK[pb]], PS[:, pb * 512 + (tl % 4) * 128:pb * 512 + (tl % 4 + 1) * 128],
                           lhsT=OJq[:, tt, :], rhs=cj[o][:], start=True, stop=True)
                        if tl % 4 == 3:
                            tq = tl % 32
                            op("act", "copy", [BK[pb]], [b_wtq[ws_]], out=WTq[ws_][:, :, tq - 3:tq + 1],
                               in_=PS[:, pb * 512:(pb + 1) * 512].rearrange("p (t i) -> p i t", t=4))
                    if g8 % 4 == 3:
                        qd = g8 // 4
                        dma_("sp", "wd%d" % ws_, [b_wtq[ws_]], [b_wd], out=Wd[l][t * 4 + qd].rearrange("j i t -> j (i t)"),
                             in_=WTq[ws_][:].rearrange("p i t -> p (i t)"))

            def merge_emit(la, lb):
                na, nb = len(la), len(lb)
                ia = ib = 0
                while ia < na or ib < nb:
                    if ib >= nb or (ia < na and ia * nb <= ib * na):
                        emit(la, 1)
                        ia += 1
                    else:
                        emit(lb, 1)
                        ib += 1

            front = deferred_front(0)
            emit(front, len(front))
            for t in range(NTILE):
                rec[0] = []
                phaseA3(t)
                lb_ = rec[0]
                rec[0] = None
                la_ = deferred_front(t + 1) if t + 1 < NTILE else []
                merge_emit(la_, lb_)
            S.barrier()

            def load_tab(ig):
                tb_ = ig % 2
                S.dma("pool", "tu%d" % tb_, [], [b_tu[tb_]], out=tabU[tb_][:],
                      in_=ut_d[l][:, ig * 1024:(ig + 1) * 1024].rearrange("(c p) e -> p c e", p=128))
                S.dma("pool", "tv%d" % tb_, [], [b_tv[tb_]], out=tabV[tb_][:],
                      in_=v_d[l][ig * 1024:(ig + 1) * 1024, :].rearrange("(i p) d -> p i d", p=128))

            units = [(ig, blk, i) for ig in range(16) for blk in range(9) for i in range(8)]
            wtn = [0]

            def geom(blk):
                s0_ = blk * 256
                n_ = 256 if blk < 8 else 128
                return s0_, n_

            def emitA(u):
                ig, blk, i = units[u]
                s0_, n_ = geom(blk)
                tb_ = ig % 2
                if i == 0:
                    wsl = wtn[0] % 2
                    wtn[0] += 1
                    nq = n_ // 32
                    S.dma("sp", "wt%d" % wsl, [b_wd], [b_wtb[wsl]], out=wtb[wsl][:, 0:nq, :, :],
                          in_=Wd[l][s0_ // 32:s0_ // 32 + nq, :, ig * 8:(ig + 1) * 8, :].rearrange("q j i t -> j q i t"))
                pa = 4 + (u % 4)
                for dc in range(8):
                    op("pe", "matmul", [b_tu[tb_]] + b_hn2all[s0_ // 128:(s0_ + n_) // 128], [BK[pa]], bank(pa, n_),
                       lhsT=tabU[tb_][:, dc, i * 128:(i + 1) * 128], rhs=hn2all[:, dc, s0_:s0_ + n_], start=(dc == 0), stop=(dc == 7))

            def emitV(u):
                ig, blk, i = units[u]
                s0_, n_ = geom(blk)
                nq = n_ // 32
                tb_ = ig % 2
                pa = 4 + (u % 4)
                g_ = u % 2
                w_ = u % 4
                wsl = (u // 8) % 2
                if blk == 0 and i == 0 and ig + 1 < 16:
                    load_tab(ig + 1)
                op("act", "activation", [BK[pa]], [b_gl[g_]], out=gl_sb[g_][:, :n_], in_=bank(pa, n_), func=AF.Gelu)
                op("dve", "tensor_tensor", [b_gl[g_], b_wtb[wsl]], [b_wa[w_]], out=WA[w_][:, :n_].rearrange("p (q t) -> p q t", t=32),
                   in0=gl_sb[g_][:, :n_].rearrange("p (q t) -> p q t", t=32), in1=wtb[wsl][:, 0:nq, i, :], op=ALU.mult)
                for dc in range(8):
                    op("pe", "matmul", [b_tv[tb_], b_wa[w_]], [BK[dc // 2]], PS[:, dc * 256:dc * 256 + n_],
                       lhsT=tabV[tb_][:, i, dc * 128:(dc + 1) * 128], rhs=WA[w_][:, :n_],
                       start=(i == 0 and dc % 2 == 0), stop=(i == 7))
                if i == 7:
                    xb_ = b_x[s0_ // 128:(s0_ + n_) // 128]
                    for dc in range(8):
                        if blk < 8:
                            op("dve", "scalar_tensor_tensor", [BK[dc // 2], b_mod] + xb_, xb_, out=xT[:, dc, s0_:s0_ + n_],
                               in0=PS[:, dc * 256:dc * 256 + n_], scalar=modL[:, 40 + dc, 0:1], in1=xT[:, dc, s0_:s0_ + n_],
                               op0=ALU.mult, op1=ALU.add)
                        else:
                            g_ = dc % 2
                            op("dve", "tensor_tensor", [BK[dc // 2], b_mod], [b_gl[g_]], out=v3(gl_sb[g_][:, :n_], 8),
                               in0=v3(PS[:, dc * 256:dc * 256 + n_], 8), in1=bc_seq(modL[:, 40 + dc, 1:17], 8), op=ALU.mult)
                            op("dve", "tensor_tensor", [b_gl[g_]] + xb_, xb_, out=xT[:, dc, s0_:s0_ + n_], in0=gl_sb[g_][:, :n_],
                               in1=xT[:, dc, s0_:s0_ + n_], op=ALU.add)

            load_tab(0)
            emitA(0)
            for u in range(len(units)):
                if u + 1 < len(units):
                    emitA(u + 1)
                emitV(u)
            S.barrier()
            continue

        pend = deferred_front(0)
        emit(pend, len(pend))
        for t in range(NTILE):
            pend = deferred_front(t + 1) if t + 1 < NTILE else []
            phaseB(t, pend)
            phaseC(t)
        S.barrier()

    S.barrier()
    frg = [0]
    for t in range(NTILE):
        s0 = t * 128
        xb = [b_x[t]]
        rms_rstd(xT[:, :, s0:s0 + 128], 128, xb, psq, b_psq, prstd, b_prstd, prtmp, b_prtmp, 0)
        for cc in range(8):
            op("dve", "scalar_tensor_tensor", xb + [b_pt, b_prstd], [b_hn2], out=hn2T[:, cc, :], in0=xT[:, cc, s0:s0 + 128],
               scalar=PT_B[:, 0, 24 + cc:25 + cc], in1=prstd[:], op0=ALU.mult, op1=ALU.mult)
        pb = 2 + (t % 2) * 2
        for cc in range(8):
            op("pe", "transpose", [b_hn2, b_ident], [BK[pb + cc // 4]], out=PS[:, pb * 512 + cc * 128: pb * 512 + (cc + 1) * 128],
               in_=hn2T[:, cc, :], identity=ident[:])
        sl = frg[0] % NG
        frg[0] += 1
        op("act", "copy", [BK[pb], BK[pb + 1]], [b_ring[sl]], out=ring[sl][:], in_=PS[:, pb * 512: pb * 512 + 1024])
        ob = Buf("y%d" % t)
        O_bufs.append(ob)
        S.dma("sp", "yout%d" % sl, [b_ring[sl]], [ob], out=y_d[s0:s0 + 128, :], in_=ring[sl][:])
    S.wait_all("sp", O_bufs)
    S.barrier()
    return nc, S, (mix_end, peer_end)


def host_prep(inp):
    f = np.float32
    g = {k: np.asarray(v) for k, v in inp.items()}
    common = {}
    common["w_ada"] = np.ascontiguousarray(g["w_ada"], f)
    common["w_in"] = np.ascontiguousarray(g["w_in"], f)
    common["w_out"] = np.ascontiguousarray(g["w_out"], f)
    common["w_q"] = np.ascontiguousarray(g["peer_w_q"], f)
    sk = g["peer_sub_keys"].astype(f)
    common["skT"] = np.ascontiguousarray(sk.reshape(L, 16, 128, 128).transpose(0, 1, 3, 2))
    pA = np.zeros((L, 128, 128), f)
    pB = np.zeros((L, 32, 128), f)
    for l in range(L):
        pA[l, 0:48] = g["b_ada"][l].reshape(48, 128)
        pA[l, 48:56] = g["norm1"][l].reshape(8, 128)
        pA[l, 56:64] = g["norm2"][l].reshape(8, 128)
        pA[l, 64:96] = g["conv_a_w"][l].reshape(32, 128)
        pA[l, 96:104] = g["conv_a_b"][l].reshape(8, 128)
        pA[l, 104:112] = g["rg_b_a"][l].reshape(8, 128)
        pA[l, 112:120] = g["rg_b_x"][l].reshape(8, 128)
        pA[l, 120:128] = g["rg_lambda"][l].reshape(8, 128)
        pB[l, 0:24] = g["conv_b_w"][l].reshape(24, 128)
        pB[l, 24:32] = g["final_norm"].reshape(8, 128)
    common["pA"] = pA
    common["pB"] = pB
    for nm, src in (("wbda", g["rg_w_a"]), ("wbdx", g["rg_w_x"])):
        w = np.zeros((L, 8, 128, 128), f)
        for c in range(8):
            w[:, c, 0:64, 0:64] = src[:, 2 * c]
            w[:, c, 64:128, 64:128] = src[:, 2 * c + 1]
        common[nm] = w
    common["ident"] = np.eye(128, dtype=f)
    common["iota16"] = np.tile(np.arange(16, dtype=f), (128, 1))
    if DENSE:
        common["iota128"] = np.tile(np.arange(128, dtype=f), (128, 1))
    for l in range(L):
        if DENSE:
            common["ut%d" % l] = np.ascontiguousarray(g["peer_u"][l].T, f)
        else:
            common["u%d" % l] = np.ascontiguousarray(g["peer_u"][l], f)
        common["v%d" % l] = np.ascontiguousarray(g["peer_v"][l], f)
    in_maps = []
    for i in range(8):
        m = dict(common)
        m["xin"] = np.ascontiguousarray(np.concatenate(
            [g["x_prompt"][i], g["x_sample"][16 * i:16 * i + 16].reshape(128, D)], axis=0), f)
        m["cin"] = np.ascontiguousarray(np.concatenate([g["c_prompt"][i:i + 1], g["c_sample"][16 * i:16 * i + 16]], axis=0), f)
        st = np.concatenate([g["state_conv_a"][:, 16 * i:16 * i + 16].reshape(L, 48, D),
                             g["state_h"][:, 16 * i:16 * i + 16],
                             g["state_conv_b"][:, 16 * i:16 * i + 16].reshape(L, 32, D)], axis=1)
        m["state"] = np.ascontiguousarray(st, f)
        in_maps.append(m)
    return in_maps


def assemble(results):
    f = np.float32
    y_p = np.zeros((8, 2048, D), f)
    y_s = np.zeros((128, 8, D), f)
    ca_p = np.zeros((L, 8, 3, D), f)
    h_p = np.zeros((L, 8, D), f)
    cb_p = np.zeros((L, 8, 2, D), f)
    ca_s = np.zeros((L, 128, 3, D), f)
    h_s = np.zeros((L, 128, D), f)
    cb_s = np.zeros((L, 128, 2, D), f)
    for i, r in enumerate(results):
        y = r["y"]
        y_p[i] = y[0:2048]
        y_s[16 * i:16 * i + 16] = y[2048:].reshape(16, 8, D)
        o = r["ost"]
        ca_s[:, 16 * i:16 * i + 16] = o[:, 0:48].reshape(L, 16, 3, D)
        h_s[:, 16 * i:16 * i + 16] = o[:, 48:64]
        cb_s[:, 16 * i:16 * i + 16] = o[:, 64:96].reshape(L, 16, 2, D)
        ca_p[:, i] = o[:, 96:99]
        h_p[:, i] = o[:, 99]
        cb_p[:, i] = o[:, 100:102]
    return (y_p, y_s, ca_p, h_p, cb_p, ca_s, h_s, cb_s)


def kernel(**inputs):
    in_maps = host_prep(inputs)
    nc, S, _ = build_program()
    res = run_bass_kernel_spmd(nc, in_maps, core_ids=list(range(8)))
    return assemble(res.results)
```

```python
import numpy as np
import concourse.bass as bass
import concourse.mybir as mybir
from concourse.bass_utils import run_bass_kernel_spmd

F32 = mybir.dt.float32
BF16 = mybir.dt.bfloat16
I32 = mybir.dt.int32
U32 = mybir.dt.uint32
ALU = mybir.AluOpType
AF = mybir.ActivationFunctionType
AX = mybir.AxisListType

L = 4
D = 1024
NT = 2176
NTILE = 17
NCH = 8
EPS = 1e-6
SEM_LIMIT = 60000
SB_BASE = 16512
SB_END = 229376
DENSE = False


class Buf:
    __slots__ = ("name", "writer", "readers")

    def __init__(self, name=""):
        self.name = name
        self.writer = None
        self.readers = {}


class Sched:
    def __init__(self, nc):
        self.nc = nc
        self.eng = {"pe": nc.tensor, "act": nc.scalar, "dve": nc.vector,
                    "pool": nc.gpsimd, "sp": nc.sync}
        self.esem = {}
        self.ecnt = {}
        self.waited = {k: {} for k in self.eng}
        self.nsem = 0
        self.dkeys = {}
        self.ninst = {k: 0 for k in self.eng}
        for k in ("pe", "act", "dve", "pool"):
            self._new_esem(k)

    def _alloc_sem(self, name):
        self.nsem += 1
        return self.nc.alloc_semaphore(name)

    def _new_esem(self, k):
        self.esem[k] = self._alloc_sem("e_%s_%d" % (k, self.nsem))
        self.ecnt[k] = 0

    def _wait(self, e, dep):
        sem, val, src = dep[0], dep[1], dep[2]
        if len(dep) > 3 and dep[3][0] is sem:
            val = 16 * dep[3][1]
        w = self.waited[e]
        if w.get(id(sem), 0) >= val:
            return
        if src == e and sem is self.esem[e]:
            val = max(val, self.ecnt[e] - 3)
        self.eng[e].wait_ge(sem, val)
        self.ninst[e] += 1
        w[id(sem)] = val

    def _deps(self, e, reads, writes):
        for b in reads:
            if b.writer is not None:
                if b.writer[2] == e and e == "pe":
                    continue
                self._wait(e, b.writer)
        for b in writes:
            if b.writer is not None and not b.readers and not (b.writer[2] == e and e == "pe"):
                self._wait(e, b.writer)
            for r in b.readers.values():
                if not (r[2] == e and e == "pe"):
                    self._wait(e, r)

    def op(self, e, meth, reads, writes, *a, soft=(), **k):
        self._deps(e, list(reads) + list(soft), writes)
        if self.ecnt[e] >= SEM_LIMIT:
            self._new_esem(e)
        ins = getattr(self.eng[e], meth)(*a, **k)
        self.ecnt[e] += 1
        self.ninst[e] += 1
        ins.then_inc(self.esem[e], 1)
        tok = (self.esem[e], self.ecnt[e], e)
        for b in reads:
            b.readers[e] = tok
        for b in writes:
            b.writer = tok
            b.readers = {}
        return ins

    def dma(self, q, key, reads, writes, meth="dma_start", **k):
        self._deps(q, reads, writes)
        st = self.dkeys.get(key)
        if st is None or st[1] >= 3700:
            st = [self._alloc_sem("d_%s_%d" % (key, self.nsem)), 0]
            self.dkeys[key] = st
        ins = getattr(self.eng[q], meth)(**k)
        self.ninst[q] += 1
        st[1] += 1
        ins.then_inc(st[0], 16)
        tok = (st[0], 16 * st[1], "dma:" + key, st)
        for b in reads:
            b.readers["dma:" + key] = tok
        for b in writes:
            b.writer = tok
            b.readers = {}
        return ins

    def wait_all(self, e, bufs):
        for b in bufs:
            if b.writer is not None:
                self._wait(e, b.writer)
            for r in b.readers.values():
                self._wait(e, r)

    def barrier(self):
        toks = []
        for k in ("pe", "act", "dve", "pool"):
            if self.ecnt[k] > 0:
                toks.append((self.esem[k], self.ecnt[k], k))
        for key, st in self.dkeys.items():
            toks.append((st[0], 16 * st[1], "dma:" + key))
        for e in self.eng:
            for t in toks:
                if t[2] == e:
                    continue
                self._wait(e, t)


class Arena:
    def __init__(self, nc, base, end):
        self.nc = nc
        self.off = base
        self.end = end

    def __call__(self, name, shape, dt=F32):
        n = 1
        for s in shape[1:]:
            n *= s
        n *= 2 if dt == BF16 else 4
        h = self.nc.alloc_sbuf_tensor_at(name, list(shape), dt, offset=self.off)
        self.off += (n + 63) // 64 * 64
        assert self.off <= self.end, ("SBUF overflow", name, self.off, self.end)
        return h


def bufs(n, name):
    return [Buf("%s%d" % (name, i)) for i in range(n)]


def build_program(n_layers=L, do_peer=True, dbg=False, dense=DENSE):
    nc = bass.Bass("TRN2", target_bir_lowering=False)
    S = Sched(nc)

    def din(n, s, dt=F32):
        return nc.dram_tensor(n, s, dt, kind="ExternalInput").ap()

    def dout(n, s, dt=F32):
        return nc.dram_tensor(n, s, dt, kind="ExternalOutput").ap()

    xin = din("xin", [NT, D])
    cin = din("cin", [17, D])
    state_d = din("state", [L, 96, D])
    w_ada = din("w_ada", [L, D, 6 * D])
    w_in = din("w_in", [L, D, 6 * D])
    w_out = din("w_out", [L, D, D])
    w_q = din("w_q", [L, D, 2048])
    skT_d = din("skT", [L, 16, 128, 128])
    pA_d = din("pA", [L, 128, 128])
    pB_d = din("pB", [L, 32, 128])
    wbda_d = din("wbda", [L, 8, 128, 128])
    wbdx_d = din("wbdx", [L, 8, 128, 128])
    ident_d = din("ident", [128, 128])
    iota_d = din("iota16", [128, 16])
    if dense:
        ut_d = [din("ut%d" % l, [D, 16384]) for l in range(L)]
        iota128_d = din("iota128", [128, 128])
        Wd = [nc.dram_tensor("Wd%d" % l, [NT // 32, 128, 128, 32], BF16, kind="Internal").ap() for l in range(L)]
        b_wd = Buf("wd")
    else:
        u_d = [din("u%d" % l, [16384, D]) for l in range(L)]
    v_d = [din("v%d" % l, [16384, D]) for l in range(L)]
    uvb_d = [nc.dram_tensor("uvb%d" % l, [16384, 2 * D], BF16, kind="Internal").ap() for l in range(L)]
    b_uv = bufs(L, "uv")
    modD = nc.dram_tensor("modD", [128, L * 48 * 17], F32, kind="Internal").ap()
    b_modD = Buf("modD")
    y_d = dout("y", [NT, D])
    ost_d = dout("ost", [L, 102, D])
    O_bufs = []

    PS = nc.alloc_psum_tensor("PS", [128, 4096], F32)
    BK = bufs(8, "bank")

    def bank(b, n=512, p=128):
        return PS[0:p, b * 512:b * 512 + n]

    A = Arena(nc, SB_BASE, SB_END)
    xT = A("xT", [128, NCH, NT])
    b_x = bufs(NTILE, "x")
    modL = A("modL", [128, 48, 17])
    b_mod = Buf("mod")
    b_modall = Buf("modall")
    PT_A = A("PT_A", [128, L, 128])
    PT_B = A("PT_B", [128, L, 32])
    b_pt = Buf("pt")
    ident = A("ident", [128, 128])
    b_ident = Buf("ident")
    ones = A("ones", [128, 128])
    b_ones = Buf("ones")
    iota = A("iota", [128, 16])
    b_iota = Buf("iota")
    negsp8 = A("negsp8", [128, L, 8])
    b_nsp = Buf("nsp")
    cT = A("cT", [128, NCH, 17])
    b_cT = Buf("cT")
    gs1p = A("gs1p", [128, 8])
    gs2p = A("gs2p", [128, 8])
    gs1s = A("gs1s", [128, 8, 16])
    gs2s = A("gs2s", [128, 8, 16])
    b_gs = Buf("gs")
    h_state = A("h_state", [128, 8])
    halo_a = A("halo_a", [128, 8, 3])
    halo_u = A("halo_u", [128, 8, 2])
    b_hst = bufs(8, "hst")
    SCR = A.off

    M = Arena(nc, SCR, SB_END)
    NSTG = 4
    stg = [M("stg%d" % i, [128, 1024]) for i in range(NSTG)]
    b_stg = bufs(NSTG, "stg")
    hnblk = M("hnblk", [128, NCH, 512], BF16)
    b_hn = bufs(8, "hn")
    NWB = 12
    modT = nc.alloc_sbuf_tensor_at("modTall", [128, L, 48, 17], F32, offset=M.off)
    wbf = [M("wbf%d" % i, [128, 8, 128], BF16) for i in range(NWB)]
    b_wbf = bufs(NWB, "wbf")
    woutb = M("woutb", [128, 8, 1024], BF16)
    b_wout = bufs(8, "wout")
    wbda = M("wbda", [128, 8, 128])
    wbdx = M("wbdx", [128, 8, 128])
    b_wbd = Buf("wbd")
    merged = M("merged", [128, NCH, 512], BF16)
    b_mg = bufs(8, "mg")
    stateT = M("stateT", [128, 8, 96])
    b_stT = Buf("stT")
    outT = M("outT", [128, 8, 102])
    b_outT = bufs(8, "outT")
    sqr = [M("sq%d" % i, [128, 512]) for i in range(2)]
    b_sq = bufs(2, "sq")
    rstd = M("rstd", [128, 512])
    b_rstd = Buf("rstd")
    rtmp = M("rtmp", [128, 512])
    b_rtmp = Buf("rtmp")
    xa_ext2 = [M("xa_ext%d" % i, [128, 528]) for i in range(2)]
    xc2 = [M("xc%d" % i, [128, 512]) for i in range(2)]
    t_r = M("t_r", [128, 512])
    t_i = M("t_i", [128, 512])
    t_m = M("t_m", [128, 512])
    t_h = M("t_h", [128, 512])
    t_xs = M("t_xs", [128, 512])
    u_ext = M("u_ext", [128, 528])
    uc2 = [M("t_uc%d" % i, [128, 512]) for i in range(2)]
    ga2 = [M("t_ga%d" % i, [128, 512]) for i in range(2)]
    gb2 = [M("t_gb%d" % i, [128, 512]) for i in range(2)]
    b_t = {k: Buf(k) for k in ["r", "i", "m", "h", "xs", "u"]}
    b_t2 = [{k: Buf(k + str(i)) for k in ["xa", "xc", "uc", "ga", "gb"]} for i in range(2)]
    mix_end = M.off

    P = Arena(nc, SCR, SB_END)
    if dense:
        hn2all = P("hn2all", [128, NCH, NT], BF16)
        b_hn2all = bufs(NTILE, "hn2all")
        DSCR = P.off
    NG = 1 if dense else 5
    ring = [P("ring%d" % i, [128, 1024]) for i in range(NG)]
    b_ring = bufs(NG, "ring")
    wqb_off = P.off
    wqb = P("wqb", [128, 8, 2048], BF16)
    b_wq = bufs(8, "wq")
    skT = P("skT", [128, 16, 128])
    b_sk = Buf("sk")
    if dense:
        hn2T = nc.alloc_sbuf_tensor_at("hn2Tfin", [128, 8, 128], F32, offset=wqb_off)
        hn2Tb = None
    else:
        hn2T = P("hn2T", [128, 8, 128])
        hn2Tb = P("hn2Tb", [128, 8, 128], BF16)
    b_hn2 = Buf("hn2")
    b_hn2b = Buf("hn2b")
    hn2toks = [None, None] if dense else [P("hn2tok%d" % i, [128, 1024], BF16) for i in range(2)]
    b_toks = bufs(2, "tok")
    qT = P("qT", [128, 8, 128])
    b_qT = Buf("qT")
    s_sb = P("s_sb", [128, 16, 128])
    b_s = bufs(16, "s")
    work = P("work", [128, 8, 128])
    b_work = bufs(8, "work")
    cand = P("cand", [128, 4, 16, 16])
    b_cand = bufs(4, "cand")
    NGA = NG if dense else 12
    for i in range(NGA - NG):
        ring.append(P("ringx%d" % i, [128, 1024]))
        b_ring.append(Buf("ringx%d" % i))
    ringb = [r[:].bitcast(BF16) for r in ring]
    NDG = 3
    dg = [None] * NDG if dense else [P("dg%d" % i, [128, 128], BF16) for i in range(NDG)]
    b_dg = bufs(NDG, "dg")
    b_actc = bufs(16, "actc")
    b_ps = bufs(16, "prodslot")
    b_glc = bufs(16, "glc")
    b_wgc = bufs(16, "wgc")
    sv = P("sv", [128, 16, 16])
    b_sv = bufs(16, "sv")
    si = P("si", [128, 16, 16], U32)
    b_si = bufs(16, "si")
    sif = P("sif", [128, 16, 16])
    b_sif = Buf("sif")
    fv = P("fv", [128, 8, 16])
    b_fv = bufs(8, "fv")
    fi = P("fi", [128, 8, 16], U32)
    b_fi = bufs(8, "fi")
    fa_f = P("fa_f", [128, 8, 16])
    fb_f = P("fb_f", [128, 8, 16])
    fa_u = fa_f[:].bitcast(U32)
    fb_u = fb_f[:].bitcast(U32)
    isel = P("isel", [128, 128])
    jsel = P("jsel", [128, 128])
    ef = None if dense else P("ef", [128, 128])
    idxs = [None, None] if dense else [P("idx%d" % i, [128, 128], I32) for i in range(2)]
    b_idxs = bufs(2, "idx")
    ex = P("ex", [128, 8, 16])
    ssum = P("ssum", [128, 8])
    rsum = P("rsum", [128, 8])
    gws = [P("gw%d" % i, [128, 8, 16]) for i in range(2)]
    b_gws = bufs(2, "gw")
    b_rt = {k: Buf(k) for k in ["fa_u", "fb_u", "fa_f", "fb_f", "isel", "jsel", "ef", "ex", "ssum", "rsum"]}
    actv = None if dense else P("actv", [128, 128])
    gl = actv
    b_actv = Buf("actv")
    wgt = None if dense else P("wgt", [128, 128])
    b_wgt = Buf("wgt")
    b_prod = Buf("prod")
    acc = hn2T[:].rearrange("p c n -> p (c n)")
    b_acc = b_hn2
    psq = [P("psq%d" % i, [128, 128]) for i in range(2)]
    b_psq = bufs(2, "psq")
    prstd = P("prstd", [128, 128])
    b_prstd = Buf("prstd")
    prtmp = P("prtmp", [128, 128])
    b_prtmp = Buf("prtmp")
    ptmp3 = hn2T
    b_ptmp3 = b_hn2
    if dense:
        tp = P("tp", [128, 3, 128])
        b_tp = Buf("tp")
        iota128 = P("iota128", [128, 128])
        b_iota128 = Buf("iota128")
        cj = [P("cj%d" % i, [128, 128], BF16) for i in range(4)]
        b_cj = bufs(4, "cj")
        OJq = P("OJq", [128, 8, 128], BF16)
        OIq = P("OIq", [128, 8, 128], BF16)
        b_ojq = Buf("ojq")
        b_oiq = Buf("oiq")
        WTq = [P("WTq%d" % i, [128, 128, 32], BF16) for i in range(2)]
        b_wtq = bufs(2, "wtq")
        Q = Arena(nc, DSCR, SB_END)
        tabU = [Q("tabU%d" % i, [128, 8, 1024], BF16) for i in range(2)]
        tabV = [Q("tabV%d" % i, [128, 8, 1024], BF16) for i in range(2)]
        b_tu = bufs(2, "tu")
        b_tv = bufs(2, "tv")
        wtb = [Q("wtb%d" % i, [128, 8, 8, 32], BF16) for i in range(2)]
        b_wtb = bufs(2, "wtb")
        gl_sb = [Q("glsb%d" % i, [128, 256]) for i in range(2)]
        b_gl = bufs(2, "glsb")
        WA = [Q("WA%d" % i, [128, 256], BF16) for i in range(4)]
        b_wa = bufs(4, "wa")
    peer_end = P.off

    rec = [None]

    def op(*a_, **k_):
        if rec[0] is not None:
            rec[0].append(("op", a_, k_))
            return None
        return S.op(*a_, **k_)

    def dma_(*a_, **k_):
        if rec[0] is not None:
            rec[0].append(("dma", a_, k_))
            return None
        return S.dma(*a_, **k_)

    def ld(q, key, dst, src, wb, rb=()):
        return S.dma(q, key, list(rb), list(wb), out=dst, in_=src)

    ld("sp", "c0", ident[:], ident_d[:, :], [b_ident])
    ld("sp", "c0", iota[:], iota_d[:, :], [b_iota])
    op("pool", "memset", [], [b_ones], ones[:], 1.0)
    stg_ctr = [0]

    def next_stg():
        i = stg_ctr[0] % NSTG
        stg_ctr[0] += 1
        return i

    for t in range(NTILE):
        sl = next_stg()
        ld("sp", "stg%d" % sl, stg[sl][:], xin[t * 128:(t + 1) * 128, :], [b_stg[sl]])
        pb = (t % 4) * 2
        for c in range(8):
            op("pe", "transpose", [b_stg[sl], b_ident], [BK[pb + c // 4]],
               out=PS[:, pb * 512 + c * 128: pb * 512 + (c + 1) * 128],
               in_=stg[sl][:, c * 128:(c + 1) * 128], identity=ident[:])
        eng = "act" if t % 2 == 0 else "dve"
        src = PS[:, pb * 512: pb * 512 + 1024].rearrange("p (c n) -> p c n", c=8)
        if eng == "act":
            op("act", "copy", [BK[pb], BK[pb + 1]], [b_x[t]], out=xT[:, :, t * 128:(t + 1) * 128], in_=src)
        else:
            op("dve", "tensor_copy", [BK[pb], BK[pb + 1]], [b_x[t]], out=xT[:, :, t * 128:(t + 1) * 128], in_=src)

    for l in range(L):
        sl = next_stg()
        ld("sp", "stg%d" % sl, stg[sl][:, 0:128], pA_d[l, :, :], [b_stg[sl]])
        ld("sp", "stg%d" % sl, stg[sl][0:32, 128:256], pB_d[l, :, :], [b_stg[sl]])
        op("pe", "transpose", [b_stg[sl], b_ident], [BK[0]], out=PS[:, 0:128], in_=stg[sl][:, 0:128], identity=ident[:])
        op("pe", "transpose", [b_stg[sl], b_ident], [BK[0]], out=PS[:, 128:160], in_=stg[sl][0:32, 128:256],
           identity=ident[0:32, 0:32])
        op("act", "copy", [BK[0]], [b_pt], out=PT_A[:, l, :], in_=PS[:, 0:128])
        op("act", "copy", [BK[0]], [b_pt], out=PT_B[:, l, :], in_=PS[:, 128:160])
    op("act", "activation", [b_pt], [b_nsp], out=negsp8[:], in_=PT_A[:, :, 120:128], func=AF.Exp, scale=-1.0)
    op("act", "activation", [b_nsp], [b_nsp], out=negsp8[:], in_=negsp8[:], func=AF.Ln, bias=1.0, scale=1.0)
    op("dve", "tensor_scalar", [b_nsp], [b_nsp], out=negsp8[:], in0=negsp8[:], scalar1=-8.0, scalar2=None, op0=ALU.mult)

    sl = next_stg()
    ld("sp", "stg%d" % sl, stg[sl][0:17, :], cin[:, :], [b_stg[sl]])
    for c in range(8):
        op("pe", "transpose", [b_stg[sl], b_ident], [BK[1]], out=PS[:, 512 + c * 17: 512 + (c + 1) * 17],
           in_=stg[sl][0:17, c * 128:(c + 1) * 128], identity=ident[0:17, 0:17])
    op("act", "copy", [BK[1]], [b_cT], out=cT[:], in_=PS[:, 512:512 + 136].rearrange("p (c n) -> p c n", c=8))
    modtok = rtmp
    it = 0
    for l in range(L):
        for cb in range(12):
            pbk = 2 + (it % 2)
            tbk = 4 + (it % 2)
            it += 1
            for k in range(8):
                sl = next_stg()
                ld("sp", "stg%d" % sl, stg[sl][:, 0:512],
                   w_ada[l, k * 128:(k + 1) * 128, cb * 512:(cb + 1) * 512], [b_stg[sl]])
                rhs = stg[sl][:, 0:512]
                op("pe", "matmul", [b_cT, b_stg[sl]], [BK[pbk]], bank(pbk, 512, 17), lhsT=cT[:, k, :], rhs=rhs,
                   start=(k == 0), stop=(k == 7))
            op("act", "copy", [BK[pbk]], [b_rtmp], out=modtok[0:17, :], in_=bank(pbk, 512, 17))
            for j in range(4):
                op("pe", "transpose", [b_rtmp, b_ident], [BK[tbk]], out=PS[:, tbk * 512 + j * 17: tbk * 512 + (j + 1) * 17],
                   in_=modtok[0:17, j * 128:(j + 1) * 128], identity=ident[0:17, 0:17])
            op("dve", "tensor_tensor", [BK[tbk], b_pt], [b_modall], out=modT[:, l, cb * 4:(cb + 1) * 4, :],
               in0=PS[:, tbk * 512: tbk * 512 + 68].rearrange("p (j n) -> p j n", j=4),
               in1=PT_A[:, l, cb * 4:(cb + 1) * 4].unsqueeze(2).broadcast_to([128, 4, 17]), op=ALU.add)

    S.dma("sp", "modD", [b_modall], [b_modD], out=modD[:, :], in_=modT[:].rearrange("p l c n -> p (l c n)"))
    S.barrier()

    blocks = [(0, 512), (512, 512), (1024, 512), (1536, 512), (2048, 128)]

    def xbufs(s0, n):
        return b_x[s0 // 128:(s0 + n) // 128]

    def v3(ap, inner):
        return ap.rearrange("p (s t) -> p s t", t=inner)

    def bc_seq(ap16, inner):
        return ap16.unsqueeze(2).broadcast_to([128, 16, inner])

    def rms_rstd(src3, n, rb, sq_t, b_sq_t, rstd_t, b_rstd_t, rtmp_t, b_rtmp_t, pbk):
        for c in range(8):
            q = c % 2
            op("act", "activation", rb, [b_sq_t[q]], out=sq_t[q][:, :n], in_=src3[:, c, :], func=AF.Square)
            op("pe", "matmul", [b_sq_t[q], b_ones], [BK[pbk]], bank(pbk, n), lhsT=ones[:], rhs=sq_t[q][:, :n],
               start=(c == 0), stop=(c == 7))
        op("act", "activation", [BK[pbk]], [b_rtmp_t], out=rtmp_t[:, :n], in_=bank(pbk, n), func=AF.Sqrt,
           bias=EPS, scale=1.0 / D)
        op("dve", "reciprocal", [b_rtmp_t], [b_rstd_t], out=rstd_t[:, :n], in_=rtmp_t[:, :n])

    for l in range(n_layers):
        for k in range(8):
            S.dma("pool", "wout", [], [b_wout[k]], out=woutb[:, k, :], in_=w_out[l, k * 128:(k + 1) * 128, :])
        if do_peer and not dense:
            for ch in range(16):
                r0 = ch * 1024
                S.dma("pool", "cv", [], [b_uv[l]], out=uvb_d[l][r0:r0 + 1024, 0:D], in_=u_d[l][r0:r0 + 1024, :])
                S.dma("pool", "cv", [], [b_uv[l]], out=uvb_d[l][r0:r0 + 1024, D:2 * D], in_=v_d[l][r0:r0 + 1024, :])
        ld("sp", "wbd", wbda[:], wbda_d[l].rearrange("c p m -> p c m"), [b_wbd])
        ld("sp", "wbd", wbdx[:], wbdx_d[l].rearrange("c p m -> p c m"), [b_wbd])
        sl = next_stg()
        ld("sp", "stg%d" % sl, stg[sl][0:96, :], state_d[l, :, :], [b_stg[sl]])
        for c in range(8):
            op("pe", "transpose", [b_stg[sl], b_ident], [BK[c // 4]], out=PS[:, c * 128:c * 128 + 96],
               in_=stg[sl][0:96, c * 128:(c + 1) * 128], identity=ident[0:96, 0:96])
        op("act", "copy", [BK[0], BK[1]], [b_stT], out=stateT[:], in_=PS[:, 0:1024].rearrange("p (c n) -> p c n", c=8)[:, :, 0:96])
        ld("sp", "modL", modL[:].rearrange("p c n -> p (c n)"), modD[:, l * 816:(l + 1) * 816], [b_mod], rb=[b_modD])
        op("dve", "scalar_tensor_tensor", [b_mod, b_pt], [b_gs], out=gs1p[:], in0=modL[:, 8:16, 0], scalar=1.0,
           in1=PT_A[:, l, 48:56], op0=ALU.add, op1=ALU.mult)
        op("dve", "scalar_tensor_tensor", [b_mod, b_pt], [b_gs], out=gs2p[:], in0=modL[:, 32:40, 0], scalar=1.0,
           in1=PT_A[:, l, 56:64], op0=ALU.add, op1=ALU.mult)
        op("dve", "scalar_tensor_tensor", [b_mod, b_pt], [b_gs], out=gs1s[:], in0=modL[:, 8:16, 1:17], scalar=1.0,
           in1=PT_A[:, l, 48:56].unsqueeze(2).broadcast_to([128, 8, 16]), op0=ALU.add, op1=ALU.mult)
        op("dve", "scalar_tensor_tensor", [b_mod, b_pt], [b_gs], out=gs2s[:], in0=modL[:, 32:40, 1:17], scalar=1.0,
           in1=PT_A[:, l, 56:64].unsqueeze(2).broadcast_to([128, 8, 16]), op0=ALU.add, op1=ALU.mult)

        iters = [(tb, c) for tb in range(5) for c in range(8)]

        def load_w(n):
            tb, c = iters[n]
            for g in range(6):
                col0 = g * 1024 + c * 128
                wi = (n % 2) * 6 + g
                S.dma("pool", "wbf%d" % wi, [], [b_wbf[wi]], out=wbf[wi][:],
                      in_=w_in[l][:, col0:col0 + 128].rearrange("(k p) m -> p k m", p=128))

        load_w(0)

        def part1(n):
            tb, c = iters[n]
            s0, N = blocks[tb]
            samp = (tb == 4)
            xb = xbufs(s0, N)
            p = n % 2
            xa_ext, xc, t_uc, t_ga, t_gb, bp = xa_ext2[p], xc2[p], uc2[p], ga2[p], gb2[p], b_t2[p]
            if c == 0:
                rms_rstd(xT[:, :, s0:s0 + N], N, xb, sqr, b_sq, rstd, b_rstd, rtmp, b_rtmp, 6)
                for cc in range(8):
                    q = cc % 2
                    if not samp:
                        op("dve", "scalar_tensor_tensor", xb + [b_gs, b_rstd], [b_sq[q]], out=sqr[q][:, :N],
                           in0=xT[:, cc, s0:s0 + N], scalar=gs1p[:, cc:cc + 1], in1=rstd[:, :N], op0=ALU.mult, op1=ALU.mult)
                        op("act", "activation", [b_sq[q], b_mod], [b_hn[cc]], out=hnblk[:, cc, :N], in_=sqr[q][:, :N],
                           func=AF.Identity, bias=modL[:, cc, 0:1], scale=1.0)
                    else:
                        op("dve", "tensor_tensor", xb + [b_rstd], [b_sq[q]], out=sqr[q][:, :N], in0=xT[:, cc, s0:s0 + N],
                           in1=rstd[:, :N], op=ALU.mult)
                        op("dve", "tensor_tensor", [b_sq[q], b_gs], [b_sq[q]], out=v3(sqr[q][:, :N], 8), in0=v3(sqr[q][:, :N], 8),
                           in1=bc_seq(gs1s[:, cc, :], 8), op=ALU.mult)
                        op("dve", "tensor_tensor", [b_sq[q], b_mod], [b_hn[cc]], out=v3(hnblk[:, cc, :N], 8),
                           in0=v3(sqr[q][:, :N], 8), in1=bc_seq(modL[:, cc, 1:17], 8), op=ALU.add)
            if n + 1 < len(iters):
                load_w(n + 1)
            ws = (n % 2) * 6

            def proj(g):
                for k in range(8):
                    op("pe", "matmul", [b_wbf[ws + g], b_hn[k]], [BK[g]], bank(g, N), lhsT=wbf[ws + g][:, k, :],
                       rhs=hnblk[:, k, :N], start=(k == 0), stop=(k == 7))
            proj(0)
            if not samp:
                XA = xa_ext[:, 3:3 + N]
                if tb == 0:
                    op("pool", "memset", [], [bp["xa"]], xa_ext[:, 0:3], 0.0)
                else:
                    op("pool", "tensor_copy", [b_hst[c]], [bp["xa"]], out=xa_ext[:, 0:3], in_=halo_a[:, c, :])
                op("act", "copy", [BK[0]], [bp["xa"]], out=XA, in_=bank(0, N))
                if tb < 3:
                    op("pool", "tensor_copy", [bp["xa"]], [b_hst[c]], out=halo_a[:, c, :], in_=xa_ext[:, N:N + 3])
                else:
                    op("pool", "tensor_copy", [bp["xa"]], [b_outT[c]], out=outT[:, c, 96:99], in_=xa_ext[:, N:N + 3])

                def xsh(k):
                    return xa_ext[:, k:k + N]
                XC = xc[:, :N]
            else:
                xa3 = xa_ext[:, 0:176].rearrange("p (s t) -> p s t", t=11)
                op("pool", "tensor_copy", [b_stT], [bp["xa"]], out=xa3[:, :, 0:3],
                   in_=stateT[:, c, 0:48].rearrange("p (s k) -> p s k", k=3))
                op("act", "copy", [BK[0]], [bp["xa"]], out=xa3[:, :, 3:11], in_=v3(bank(0, N), 8))
                op("pool", "tensor_copy", [bp["xa"]], [b_outT[c]], out=outT[:, c, 0:48].rearrange("p (s k) -> p s k", k=3),
                   in_=xa3[:, :, 8:11])

                def xsh(k):
                    return xa3[:, :, k:k + 8]
                XC = v3(xc[:, :N], 8)
            op("act", "activation", [bp["xa"], b_pt], [bp["xc"]], out=XC, in_=xsh(0), func=AF.Identity,
               scale=PT_A[:, l, 64 + c:65 + c], bias=PT_A[:, l, 96 + c:97 + c])
            for k in range(1, 4):
                op("dve", "scalar_tensor_tensor", [bp["xa"], b_pt, bp["xc"]], [bp["xc"]], out=XC, in0=xsh(k),
                   scalar=PT_A[:, l, 64 + k * 8 + c:65 + k * 8 + c], in1=XC, op0=ALU.mult, op1=ALU.add)
            proj(3)
            op("act", "copy", [BK[3]], [b_t["xs"]], out=t_xs[:, :N], in_=bank(3, N))
            proj(2)
            if not samp:
                if tb == 0:
                    op("pool", "memset", [], [b_t["u"]], u_ext[:, 0:2], 0.0)
                else:
                    op("pool", "tensor_copy", [b_hst[c]], [b_t["u"]], out=u_ext[:, 0:2], in_=halo_u[:, c, :])
                op("dve", "tensor_tensor", [BK[2], b_t["xs"]], [b_t["u"]], out=u_ext[:, 2:2 + N], in0=bank(2, N), in1=t_xs[:, :N],
                   op=ALU.mult)
                if tb < 3:
                    op("pool", "tensor_copy", [b_t["u"]], [b_hst[c]], out=halo_u[:, c, :], in_=u_ext[:, N:N + 2])
                else:
                    op("pool", "tensor_copy", [b_t["u"]], [b_outT[c]], out=outT[:, c, 100:102], in_=u_ext[:, N:N + 2])

                def ush(k):
                    return u_ext[:, k:k + N]
                UC = t_uc[:, :N]
                BB = bank(1, N)
            else:
                u3 = u_ext[:, 0:160].rearrange("p (s t) -> p s t", t=10)
                op("pool", "tensor_copy", [b_stT], [b_t["u"]], out=u3[:, :, 0:2],
                   in_=stateT[:, c, 64:96].rearrange("p (s k) -> p s k", k=2))
                op("dve", "tensor_tensor", [BK[2], b_t["xs"]], [b_t["u"]], out=u3[:, :, 2:10], in0=v3(bank(2, N), 8),
                   in1=v3(t_xs[:, :N], 8), op=ALU.mult)
                op("pool", "tensor_copy", [b_t["u"]], [b_outT[c]], out=outT[:, c, 64:96].rearrange("p (s k) -> p s k", k=2),
                   in_=u3[:, :, 8:10])

                def ush(k):
                    return u3[:, :, k:k + 8]
                UC = v3(t_uc[:, :N], 8)
                BB = v3(bank(1, N), 8)
            op("act", "activation", [b_t["u"], b_pt], [bp["uc"]], out=UC, in_=ush(0), func=AF.Copy, scale=PT_B[:, l, c:c + 1])
            for k in range(1, 3):
                op("dve", "scalar_tensor_tensor", [b_t["u"], b_pt, bp["uc"]], [bp["uc"]], out=UC, in0=ush(k),
                   scalar=PT_B[:, l, k * 8 + c:k * 8 + c + 1], in1=UC, op0=ALU.mult, op1=ALU.add)
            proj(1)
            op("dve", "tensor_tensor", [BK[1], bp["uc"]], [bp["uc"]], out=UC, in0=BB, in1=UC, op=ALU.mult)
            proj(4)
            op("act", "activation", [BK[4]], [bp["ga"]], out=t_ga[:, :N], in_=bank(4, N), func=AF.Sigmoid)
            proj(5)
            op("act", "activation", [BK[5]], [bp["gb"]], out=t_gb[:, :N], in_=bank(5, N), func=AF.Sigmoid)

        def part2(n):
            tb, c = iters[n]
            s0, N = blocks[tb]
            samp = (tb == 4)
            xb = xbufs(s0, N)
            p = n % 2
            xc, t_uc, t_ga, t_gb, bp = xc2[p], uc2[p], ga2[p], gb2[p], b_t2[p]
            op("pe", "matmul", [b_wbd, bp["xc"]], [BK[6]], bank(6, N), lhsT=wbda[:, c, :], rhs=xc[:, :N], start=True, stop=True)
            op("pe", "matmul", [b_wbd, bp["xc"]], [BK[7]], bank(7, N), lhsT=wbdx[:, c, :], rhs=xc[:, :N], start=True, stop=True)
            op("act", "activation", [BK[6], b_pt], [b_t["r"]], out=t_r[:, :N], in_=bank(6, N), func=AF.Sigmoid,
               bias=PT_A[:, l, 104 + c:105 + c], scale=1.0)
            op("act", "activation", [BK[7], b_pt], [b_t["i"]], out=t_i[:, :N], in_=bank(7, N), func=AF.Sigmoid,
               bias=PT_A[:, l, 112 + c:113 + c], scale=1.0)
            op("act", "activation", [b_t["r"], b_nsp], [b_t["r"]], out=t_r[:, :N], in_=t_r[:, :N], func=AF.Exp,
               scale=negsp8[:, l, c:c + 1])
            op("act", "activation", [b_t["r"]], [b_t["m"]], out=t_m[:, :N], in_=t_r[:, :N], func=AF.Square)
            op("act", "activation", [b_t["m"]], [b_t["m"]], out=t_m[:, :N], in_=t_m[:, :N], func=AF.Sqrt, bias=1.0, scale=-1.0)
            if tb == 0:
                op("pool", "memset", [b_t["m"]], [b_t["m"]], t_m[:, 0:1], 1.0)
            op("dve", "tensor_tensor", [b_t["i"], bp["xc"]], [b_t["i"]], out=t_i[:, :N], in0=t_i[:, :N], in1=xc[:, :N], op=ALU.mult)
            op("dve", "tensor_tensor", [b_t["i"], b_t["m"]], [b_t["i"]], out=t_i[:, :N], in0=t_i[:, :N], in1=t_m[:, :N], op=ALU.mult)
            if not samp:
                if tb == 0:
                    init = 0.0
                    rdi = []
                else:
                    init = h_state[:, c:c + 1]
                    rdi = [b_hst[c]]
                op("dve", "tensor_tensor_scan", [b_t["r"], b_t["i"]] + rdi, [b_t["h"]], out=t_h[:, :N], data0=t_r[:, :N],
                   data1=t_i[:, :N], initial=init, op0=ALU.mult, op1=ALU.add)
                if tb < 3:
                    op("pool", "tensor_copy", [b_t["h"]], [b_hst[c]], out=h_state[:, c:c + 1], in_=t_h[:, N - 1:N])
                else:
                    op("pool", "tensor_copy", [b_t["h"]], [b_outT[c]], out=outT[:, c, 99:100], in_=t_h[:, N - 1:N])
            else:
                a3 = v3(t_r[:, :N], 8)
                b3 = v3(t_i[:, :N], 8)
                op("dve", "tensor_tensor", [b_t["r"], b_stT], [b_t["m"]], out=t_m[:, 0:16], in0=a3[:, :, 0],
                   in1=stateT[:, c, 48:64], op=ALU.mult)
                op("dve", "tensor_tensor", [b_t["i"], b_t["m"]], [b_t["i"]], out=b3[:, :, 0], in0=b3[:, :, 0], in1=t_m[:, 0:16],
                   op=ALU.add)
                op("dve", "memset", [], [b_t["r"]], a3[:, :, 0], 0.0)
                op("dve", "tensor_tensor_scan", [b_t["r"], b_t["i"]], [b_t["h"]], out=t_h[:, :N], data0=t_r[:, :N],
                   data1=t_i[:, :N], initial=0.0, op0=ALU.mult, op1=ALU.add)
                op("pool", "tensor_copy", [b_t["h"]], [b_outT[c]], out=outT[:, c, 48:64], in_=v3(t_h[:, :N], 8)[:, :, 7])
            op("dve", "tensor_tensor", [bp["ga"], b_t["h"]], [bp["ga"]], out=t_ga[:, :N], in0=t_ga[:, :N], in1=t_h[:, :N], op=ALU.mult)
            op("dve", "tensor_tensor", [bp["gb"], bp["uc"]], [bp["gb"]], out=t_gb[:, :N], in0=t_gb[:, :N], in1=t_uc[:, :N],
               op=ALU.mult)
            op("dve", "tensor_tensor", [bp["ga"], bp["gb"]], [b_mg[c]], out=merged[:, c, :N], in0=t_ga[:, :N], in1=t_gb[:, :N],
               op=ALU.add)
            if c == 7:
                for oc in range(8):
                    pbk = 6 + (oc % 2)
                    for k in range(8):
                        op("pe", "matmul", [b_wout[k], b_mg[k]], [BK[pbk]], bank(pbk, N), lhsT=woutb[:, k, oc * 128:(oc + 1) * 128],
                           rhs=merged[:, k, :N], start=(k == 0), stop=(k == 7))
                    if not samp:
                        op("dve", "scalar_tensor_tensor", [BK[pbk], b_mod] + xb, xb, out=xT[:, oc, s0:s0 + N], in0=bank(pbk, N),
                           scalar=modL[:, 16 + oc, 0:1], in1=xT[:, oc, s0:s0 + N], op0=ALU.mult, op1=ALU.add)
                    else:
                        op("dve", "tensor_tensor", [BK[pbk], b_mod], [b_rtmp], out=v3(rtmp[:, :N], 8), in0=v3(bank(pbk, N), 8),
                           in1=bc_seq(modL[:, 16 + oc, 1:17], 8), op=ALU.mult)
                        op("dve", "tensor_tensor", [b_rtmp] + xb, xb, out=xT[:, oc, s0:s0 + N], in0=rtmp[:, :N],
                           in1=xT[:, oc, s0:s0 + N], op=ALU.add)

        part1(0)
        for n in range(len(iters)):
            if n + 1 < len(iters) and iters[n][1] != 7:
                part1(n + 1)
                part2(n)
            else:
                part2(n)
                if n + 1 < len(iters):
                    part1(n + 1)
        for c in range(8):
            op("pe", "transpose", [b_outT[c], b_ident], [BK[c // 4]], out=PS[0:102, c * 128:(c + 1) * 128], in_=outT[:, c, :],
               identity=ident[:])
        sl = next_stg()
        op("act", "copy", [BK[0], BK[1]], [b_stg[sl]], out=stg[sl][0:102, :], in_=PS[0:102, 0:1024])
        ob = Buf("ost%d" % l)
        O_bufs.append(ob)
        S.dma("sp", "ost", [b_stg[sl]], [ob], out=ost_d[l, :, :], in_=stg[sl][0:102, :])

        if not do_peer:
            continue
        S.barrier()
        rg = [0]

        def next_ring():
            i = rg[0] % NG
            rg[0] += 1
            return i

        for k in range(8):
            S.dma("pool", "wq", [], [b_wq[k]], out=wqb[:, k, :], in_=w_q[l, k * 128:(k + 1) * 128, :])
        ld("sp", "sk", skT[:], skT_d[l].rearrange("h p k -> p h k"), [b_sk])
        def phaseA1(t):
            s0 = t * 128
            samp = (t == 16)
            xb = [b_x[t]]
            hn2tok = hn2toks[t % 2]
            b_tok = b_toks[t % 2]
            idx = idxs[t % 2]
            b_idx = b_idxs[t % 2]
            gw = gws[t % 2]
            b_gw = b_gws[t % 2]
            ab = 2 if t % 2 == 0 else 6
            rms_rstd(xT[:, :, s0:s0 + 128], 128, xb, psq, b_psq, prstd, b_prstd, prtmp, b_prtmp, 0)
            hb_ = b_hn2all[t] if dense else b_hn2

            def hdst(cc_):
                return hn2all[:, cc_, s0:s0 + 128] if dense else hn2T[:, cc_, :]
            for cc in range(8):
                q = cc % 2
                if not samp:
                    op("dve", "scalar_tensor_tensor", xb + [b_gs, b_prstd], [b_psq[q]], out=psq[q][:], in0=xT[:, cc, s0:s0 + 128],
                       scalar=gs2p[:, cc:cc + 1], in1=prstd[:], op0=ALU.mult, op1=ALU.mult)
                    op("act", "activation", [b_psq[q], b_mod], [hb_], out=hdst(cc), in_=psq[q][:], func=AF.Identity,
                       bias=modL[:, 24 + cc, 0:1], scale=1.0)
                else:
                    op("dve", "tensor_tensor", xb + [b_prstd], [b_psq[q]], out=psq[q][:], in0=xT[:, cc, s0:s0 + 128], in1=prstd[:],
                       op=ALU.mult)
                    op("dve", "tensor_tensor", [b_psq[q], b_gs], [b_psq[q]], out=v3(psq[q][:], 8), in0=v3(psq[q][:], 8),
                       in1=bc_seq(gs2s[:, cc, :], 8), op=ALU.mult)
                    op("dve", "tensor_tensor", [b_psq[q], b_mod], [hb_], out=v3(hdst(cc), 8), in0=v3(psq[q][:], 8),
                       in1=bc_seq(modL[:, 24 + cc, 1:17], 8), op=ALU.add)
            if dense:
                qrhs = hn2all[:, :, s0:s0 + 128]
                qrb = b_hn2all[t]
            else:
                op("act", "copy", [b_hn2], [b_hn2b], out=hn2Tb[:], in_=hn2T[:])
                qrhs = hn2Tb
                qrb = b_hn2b
                for cc in range(8):
                    op("pe", "transpose", [b_hn2, b_ident], [BK[cc // 4]], out=PS[:, cc * 128:(cc + 1) * 128],
                       in_=hn2T[:, cc, :], identity=ident[:])
                op("act", "copy", [BK[0], BK[1]], [b_tok], out=hn2tok[:], in_=PS[:, 0:1024])
            for g8 in range(2):
                for hp in range(g8 * 8, g8 * 8 + 8):
                    h8 = hp % 8
                    for k in range(8):
                        op("pe", "matmul", [b_wq[k], qrb], [BK[4 + h8 // 4]], PS[:, 2048 + h8 * 128: 2048 + (h8 + 1) * 128],
                           lhsT=wqb[:, k, hp * 128:(hp + 1) * 128], rhs=qrhs[:, k, :], start=(k == 0), stop=(k == 7))
                op("act", "copy", BK[4:6], [b_qT], out=qT[:].rearrange("p h n -> p (h n)"), in_=PS[:, 2048:3072])
                for hp in range(g8 * 8, g8 * 8 + 8):
                    h8 = hp % 8
                    op("pe", "matmul", [b_qT, b_sk], [BK[4 + h8 // 4]], PS[:, 2048 + h8 * 128: 2048 + (h8 + 1) * 128], lhsT=qT[:, h8, :],
                       rhs=skT[:, hp, :], start=True, stop=True)
                op("act", "copy", BK[4:6], b_s[g8 * 8:(g8 + 1) * 8], out=s_sb[:, g8 * 8:(g8 + 1) * 8, :].rearrange("p h n -> p (h n)"),
                   in_=PS[:, 2048:3072])
        def phaseA2(t):
            idx = idxs[t % 2]
            b_idx = b_idxs[t % 2]
            gw = gws[t % 2]
            b_gw = b_gws[t % 2]
            ab = 2 if t % 2 == 0 else 6
            ops = []

            def q(e, meth, R, W, *a_, **k_):
                ops.append(("op", (e, meth, list(R), list(W)) + a_, k_))
            for g8 in range(2):
                hps = range(g8 * 8, g8 * 8 + 8)
                for hp in hps:
                    q("dve", "max", [b_s[hp]], [b_sv[hp]], out=sv[:, hp, 0:8], in_=s_sb[:, hp, :])
                for hp in hps:
                    q("dve", "max_index", [b_s[hp], b_sv[hp]], [b_si[hp]], out=si[:, hp, 0:8], in_max=sv[:, hp, 0:8],
                      in_values=s_sb[:, hp, :])
                for hp in hps:
                    q("dve", "match_replace", [b_s[hp], b_sv[hp]], [b_work[hp % 8]], out=work[:, hp % 8, :],
                      in_to_replace=sv[:, hp, 0:8], in_values=s_sb[:, hp, :], imm_value=-1e30)
                for hp in hps:
                    q("dve", "max", [b_work[hp % 8]], [b_sv[hp]], out=sv[:, hp, 8:16], in_=work[:, hp % 8, :])
                for hp in hps:
                    q("dve", "max_index", [b_work[hp % 8], b_sv[hp]], [b_si[hp]], out=si[:, hp, 8:16], in_max=sv[:, hp, 8:16],
                      in_values=work[:, hp % 8, :])
            q("dve", "tensor_copy", b_si, [b_sif], out=sif[:], in_=si[:])
            svv = sv[:].rearrange("p (h t) k -> p h t k", t=2)
            sfv = sif[:].rearrange("p (h t) k -> p h t k", t=2)
            cflat = cand[:].rearrange("p h a b -> p h (a b)")
            wflat = work[:].rearrange("p (h t) n -> p h (t n)", t=2)
            for g4 in range(2):
                hs = range(g4 * 4, g4 * 4 + 4)
                h0 = g4 * 4
                q("dve", "tensor_tensor", b_sv[2 * h0:2 * h0 + 8], b_cand, out=cand[:],
                  in0=svv[:, h0:h0 + 4, 0, :].unsqueeze(3).broadcast_to([128, 4, 16, 16]),
                  in1=svv[:, h0:h0 + 4, 1, :].unsqueeze(2).broadcast_to([128, 4, 16, 16]), op=ALU.add)
                for h in hs:
                    q("dve", "max", [b_cand[h % 4]], [b_fv[h]], out=fv[:, h, 0:8], in_=cflat[:, h % 4, :])
                for h in hs:
                    q("dve", "max_index", [b_cand[h % 4], b_fv[h]], [b_fi[h]], out=fi[:, h, 0:8], in_max=fv[:, h, 0:8],
                      in_values=cflat[:, h % 4, :])
                for h in hs:
                    q("dve", "match_replace", [b_cand[h % 4], b_fv[h]], [b_work[2 * (h % 4)], b_work[2 * (h % 4) + 1]],
                      out=wflat[:, h % 4, :], in_to_replace=fv[:, h, 0:8], in_values=cflat[:, h % 4, :], imm_value=-1e30)
                for h in hs:
                    q("dve", "max", [b_work[2 * (h % 4)], b_work[2 * (h % 4) + 1]], [b_fv[h]], out=fv[:, h, 8:16],
                      in_=wflat[:, h % 4, :])
                for h in hs:
                    q("dve", "max_index", [b_work[2 * (h % 4)], b_work[2 * (h % 4) + 1], b_fv[h]], [b_fi[h]], out=fi[:, h, 8:16],
                      in_max=fv[:, h, 8:16], in_values=wflat[:, h % 4, :])
            q("dve", "tensor_single_scalar", b_fi, [b_rt["fa_f"]], out=fa_u, in_=fi[:], scalar=4, op=ALU.logical_shift_right)
            q("dve", "tensor_single_scalar", b_fi, [b_rt["fb_f"]], out=fb_u, in_=fi[:], scalar=15, op=ALU.bitwise_and)
            q("dve", "tensor_copy", [b_rt["fa_f"]], [b_rt["fa_f"]], out=fa_f[:], in_=fa_u)
            q("dve", "tensor_copy", [b_rt["fb_f"]], [b_rt["fb_f"]], out=fb_f[:], in_=fb_u)
            iota4 = iota[:].unsqueeze(1).unsqueeze(1).broadcast_to([128, 4, 16, 16])
            for (ff_, half, dst, dk, fk) in ((fa_f, 0, isel, "isel", "fa_f"), (fb_f, 1, jsel, "jsel", "fb_f")):
                for g4 in range(2):
                    h0 = g4 * 4
                    q("dve", "tensor_tensor", [b_rt[fk], b_iota], b_cand, out=cand[:],
                      in0=ff_[:, h0:h0 + 4, :].unsqueeze(3).broadcast_to([128, 4, 16, 16]), in1=iota4, op=ALU.is_equal)
                    q("dve", "tensor_tensor", b_cand + [b_sif], b_cand, out=cand[:], in0=cand[:],
                      in1=sfv[:, h0:h0 + 4, half, :].unsqueeze(2).broadcast_to([128, 4, 16, 16]), op=ALU.mult)
                    q("dve", "tensor_reduce", b_cand, [b_rt[dk]], out=dst[:, h0 * 16:(h0 + 4) * 16],
                      in_=cand[:].rearrange("p h k a -> p (h k) a"), axis=AX.X, op=ALU.add)
            if not dense:
                q("dve", "scalar_tensor_tensor", [b_rt["isel"], b_rt["jsel"]], [b_rt["ef"]], out=ef[:], in0=isel[:], scalar=128.0,
                  in1=jsel[:], op0=ALU.mult, op1=ALU.add)
                q("dve", "tensor_copy", [b_rt["ef"]], [b_idx], out=idx[:], in_=ef[:])
            q("dve", "tensor_tensor", b_fv, [b_rt["ex"]], out=ex[:], in0=fv[:], in1=fv[:, :, 0:1].broadcast_to([128, 8, 16]),
              op=ALU.subtract)
            q("act", "activation", [b_rt["ex"]], [b_rt["ex"]], out=ex[:], in_=ex[:], func=AF.Exp)
            q("dve", "tensor_reduce", [b_rt["ex"]], [b_rt["ssum"]], out=ssum[:], in_=ex[:], axis=AX.X, op=ALU.add)
            q("dve", "reciprocal", [b_rt["ssum"]], [b_rt["rsum"]], out=rsum[:], in_=ssum[:])
            q("dve", "tensor_tensor", [b_rt["ex"], b_rt["rsum"]], [b_gw], out=gw[:], in0=ex[:],
              in1=rsum[:].unsqueeze(2).broadcast_to([128, 8, 16]), op=ALU.mult)
            return ops

        def emit(ops, n):
            for _ in range(min(n, len(ops))):
                kind_, a_, k_ = ops.pop(0)
                if kind_ == "op":
                    S.op(*a_, **k_)
                else:
                    S.dma(*a_, **k_)

        def deferred_front(t):
            rec[0] = []
            phaseA1(t)
            ops = rec[0]
            rec[0] = None
            return ops + phaseA2(t)

        def op_weight(item):
            out_ = item[2].get("out")
            n_ = 1
            if out_ is not None:
                for d_ in out_.shape[1:]:
                    n_ *= d_
            return max(1.0, n_ / 256.0)

        def emit_budget(ops, budget):
            spent = 0.0
            while ops and spent < budget:
                spent += op_weight(ops[0])
                emit(ops, 1)

        def phaseB(t, pend):
            budget = sum(op_weight(it_) for it_ in pend) / 112.0 if pend else 0.0
            s0 = t * 128
            samp = (t == 16)
            xb = [b_x[t]]
            hn2tok = hn2toks[t % 2]
            b_tok = b_toks[t % 2]
            idx = idxs[t % 2]
            b_idx = b_idxs[t % 2]
            gw = gws[t % 2]
            b_gw = b_gws[t % 2]
            ab = 2 if t % 2 == 0 else 6
            gwf = gw[:].rearrange("p h k -> p (h k)")
            LAG = 2
            slots = []

            def tail(hk_):
                sl_, rb_ = slots[hk_]
                q4_ = hk_ % 16
                d8_ = hk_ % NDG
                op("dve", "tensor_scalar", [b_glc[q4_], b_gw, b_ident], [b_dg[d8_]], out=dg[d8_][:], in0=ident[:],
                   scalar1=gl[:, hk_:hk_ + 1], scalar2=gwf[:, hk_:hk_ + 1], op0=ALU.mult, op1=ALU.mult)
                for j in range(2):
                    op("pe", "matmul", [b_dg[d8_], b_ring[sl_]], [BK[ab + j]], bank(ab + j), lhsT=dg[d8_][:],
                       rhs=rb_[:, D + 512 * j:D + 512 * (j + 1)], start=(hk_ == 0), stop=(hk_ == 127))

            for hk in range(128):
                sl = rg[0] % NGA
                rg[0] += 1
                rb = ringb[sl]
                slots.append((sl, rb))
                q4 = hk % 16
                S.dma("pool", "ring%d" % sl, [b_idx, b_uv[l]], [b_ring[sl]], meth="indirect_dma_start", out=rb, out_offset=None,
                      in_=uvb_d[l][:, :], in_offset=bass.IndirectOffsetOnAxis(ap=idx[:, hk:hk + 1], axis=0))
                op("dve", "tensor_tensor", [b_tok], [b_ps[sl]], out=rb[:, 0:D], in0=rb[:, 0:D], in1=hn2tok[:], op=ALU.mult,
                   soft=[b_ring[sl]])
                op("act", "activation", [b_ps[sl]], [b_actc[q4]], out=rb[:, 0:D], in_=rb[:, 0:D], func=AF.Copy,
                   accum_out=actv[:, hk:hk + 1])
                op("act", "activation", [b_actc[q4]], [b_glc[q4]], out=gl[:, hk:hk + 1], in_=actv[:, hk:hk + 1], func=AF.Gelu)
                if hk >= LAG:
                    tail(hk - LAG)
                emit_budget(pend, budget)
            for hk in range(128 - LAG, 128):
                tail(hk)
            emit(pend, len(pend))
        def phaseC(t):
            s0 = t * 128
            samp = (t == 16)
            xb = [b_x[t]]
            hn2tok = hn2toks[t % 2]
            b_tok = b_toks[t % 2]
            idx = idxs[t % 2]
            b_idx = b_idxs[t % 2]
            gw = gws[t % 2]
            b_gw = b_gws[t % 2]
            ab = 2 if t % 2 == 0 else 6
            op("act", "copy", [BK[ab], BK[ab + 1]], [b_acc], out=acc, in_=PS[:, ab * 512:ab * 512 + 1024])
            for cc in range(8):
                op("pe", "transpose", [b_acc, b_ident], [BK[cc // 4]], out=PS[:, cc * 128:(cc + 1) * 128],
                   in_=acc[:, cc * 128:(cc + 1) * 128], identity=ident[:])
            pv = PS[:, 0:1024].rearrange("p (c n) -> p c n", c=8)
            if not samp:
                for cc in range(8):
                    op("dve", "scalar_tensor_tensor", [BK[cc // 4], b_mod] + xb, xb, out=xT[:, cc, s0:s0 + 128], in0=pv[:, cc, :],
                       scalar=modL[:, 40 + cc, 0:1], in1=xT[:, cc, s0:s0 + 128], op0=ALU.mult, op1=ALU.add)
            else:
                for cc in range(8):
                    op("dve", "tensor_tensor", [BK[cc // 4], b_mod], [b_ptmp3], out=v3(ptmp3[:, cc, :], 8), in0=v3(pv[:, cc, :], 8),
                       in1=bc_seq(modL[:, 40 + cc, 1:17], 8), op=ALU.mult)
                op("dve", "tensor_tensor", [b_ptmp3] + xb, xb, out=xT[:, :, s0:s0 + 128], in0=ptmp3[:], in1=xT[:, :, s0:s0 + 128],
                   op=ALU.add)


        if dense:
            ld("sp", "c0", iota128[:], iota128_d[:, :], [b_iota128])
            wc = [0]
            occ = [0]

            iota3 = iota128[:].unsqueeze(1).broadcast_to([128, 8, 128])

            def phaseA3(t):
                gw = gws[t % 2]
                b_gw = b_gws[t % 2]
                srcs = ((isel[:], b_rt["isel"]), (jsel[:], b_rt["jsel"]), (gw[:].rearrange("p h k -> p (h k)"), b_gw))
                for n_, (sap, sb_) in enumerate(srcs):
                    op("pe", "transpose", [sb_, b_ident], [BK[1]], out=PS[:, 512 + n_ * 128:512 + (n_ + 1) * 128], in_=sap,
                       identity=ident[:])
                op("act", "copy", [BK[1]], [b_tp], out=tp[:].rearrange("p n t -> p (n t)"), in_=PS[:, 512:896])
                for g8 in range(16):
                    t0_ = g8 * 8
                    if g8 % 4 == 0:
                        ws_ = wc[0] % 2
                        wc[0] += 1
                    op("dve", "tensor_tensor", [b_tp, b_iota128], [b_ojq], out=OJq[:], in0=iota3,
                       in1=tp[:, 1, t0_:t0_ + 8].unsqueeze(2).broadcast_to([128, 8, 128]), op=ALU.is_equal)
                    op("dve", "tensor_tensor", [b_tp, b_iota128], [b_oiq], out=OIq[:], in0=iota3,
                       in1=tp[:, 0, t0_:t0_ + 8].unsqueeze(2).broadcast_to([128, 8, 128]), op=ALU.is_equal)
                    for tt in range(8):
                        tl = t0_ + tt
                        o = occ[0] % 4
                        occ[0] += 1
                        op("act", "activation", [b_oiq, b_tp], [b_cj[o]], out=cj[o][:], in_=OIq[:, tt, :], func=AF.Copy,
                           scale=tp[:, 2, tl:tl + 1])
                        pb = 2 + ((tl // 4) % 2)
                        op("pe", "matmul", [b_ojq, b_cj[o]], [BK[pb]], PS[:, pb * 512 + (tl % 4) * 128:pb * 512 + (tl % 4 + 1) * 128],
                           lhsT=OJq[:, tt, :], rhs=cj[o][:], start=True, stop=True)
                        if tl % 4 == 3:
                            tq = tl % 32
                            op("act", "copy", [BK[pb]], [b_wtq[ws_]], out=WTq[ws_][:, :, tq - 3:tq + 1],
                               in_=PS[:, pb * 512:(pb + 1) * 512].rearrange("p (t i) -> p i t", t=4))
                    if g8 % 4 == 3:
                        qd = g8 // 4
                        dma_("sp", "wd%d" % ws_, [b_wtq[ws_]], [b_wd], out=Wd[l][t * 4 + qd].rearrange("j i t -> j (i t)"),
                             in_=WTq[ws_][:].rearrange("p i t -> p (i t)"))

            def merge_emit(la, lb):
                na, nb = len(la), len(lb)
                ia = ib = 0
                while ia < na or ib < nb:
                    if ib >= nb or (ia < na and ia * nb <= ib * na):
                        emit(la, 1)
                        ia += 1
                    else:
                        emit(lb, 1)
                        ib += 1

            front = deferred_front(0)
            emit(front, len(front))
            for t in range(NTILE):
                rec[0] = []
                phaseA3(t)
                lb_ = rec[0]
                rec[0] = None
                la_ = deferred_front(t + 1) if t + 1 < NTILE else []
                merge_emit(la_, lb_)
            S.barrier()

            def load_tab(ig):
                tb_ = ig % 2
                S.dma("pool", "tu%d" % tb_, [], [b_tu[tb_]], out=tabU[tb_][:],
                      in_=ut_d[l][:, ig * 1024:(ig + 1) * 1024].rearrange("(c p) e -> p c e", p=128))
                S.dma("pool", "tv%d" % tb_, [], [b_tv[tb_]], out=tabV[tb_][:],
                      in_=v_d[l][ig * 1024:(ig + 1) * 1024, :].rearrange("(i p) d -> p i d", p=128))

            units = [(ig, blk, i) for ig in range(16) for blk in range(9) for i in range(8)]
            wtn = [0]

            def geom(blk):
                s0_ = blk * 256
                n_ = 256 if blk < 8 else 128
                return s0_, n_

            def emitA(u):
                ig, blk, i = units[u]
                s0_, n_ = geom(blk)
                tb_ = ig % 2
                if i == 0:
                    wsl = wtn[0] % 2
                    wtn[0] += 1
                    nq = n_ // 32
                    S.dma("sp", "wt%d" % wsl, [b_wd], [b_wtb[wsl]], out=wtb[wsl][:, 0:nq, :, :],
                          in_=Wd[l][s0_ // 32:s0_ // 32 + nq, :, ig * 8:(ig + 1) * 8, :].rearrange("q j i t -> j q i t"))
                pa = 4 + (u % 4)
                for dc in range(8):
                    op("pe", "matmul", [b_tu[tb_]] + b_hn2all[s0_ // 128:(s0_ + n_) // 128], [BK[pa]], bank(pa, n_),
                       lhsT=tabU[tb_][:, dc, i * 128:(i + 1) * 128], rhs=hn2all[:, dc, s0_:s0_ + n_], start=(dc == 0), stop=(dc == 7))

            def emitV(u):
                ig, blk, i = units[u]
                s0_, n_ = geom(blk)
                nq = n_ // 32
                tb_ = ig % 2
                pa = 4 + (u % 4)
                g_ = u % 2
                w_ = u % 4
                wsl = (u // 8) % 2
                if blk == 0 and i == 0 and ig + 1 < 16:
                    load_tab(ig + 1)
                op("act", "activation", [BK[pa]], [b_gl[g_]], out=gl_sb[g_][:, :n_], in_=bank(pa, n_), func=AF.Gelu)
                op("dve", "tensor_tensor", [b_gl[g_], b_wtb[wsl]], [b_wa[w_]], out=WA[w_][:, :n_].rearrange("p (q t) -> p q t", t=32),
                   in0=gl_sb[g_][:, :n_].rearrange("p (q t) -> p q t", t=32), in1=wtb[wsl][:, 0:nq, i, :], op=ALU.mult)
                for dc in range(8):
                    op("pe", "matmul", [b_tv[tb_], b_wa[w_]], [BK[dc // 2]], PS[:, dc * 256:dc * 256 + n_],
                       lhsT=tabV[tb_][:, i, dc * 128:(dc + 1) * 128], rhs=WA[w_][:, :n_],
                       start=(i == 0 and dc % 2 == 0), stop=(i == 7))
                if i == 7:
                    xb_ = b_x[s0_ // 128:(s0_ + n_) // 128]
                    for dc in range(8):
                        if blk < 8:
                            op("dve", "scalar_tensor_tensor", [BK[dc // 2], b_mod] + xb_, xb_, out=xT[:, dc, s0_:s0_ + n_],
                               in0=PS[:, dc * 256:dc * 256 + n_], scalar=modL[:, 40 + dc, 0:1], in1=xT[:, dc, s0_:s0_ + n_],
                               op0=ALU.mult, op1=ALU.add)
                        else:
                            g_ = dc % 2
                            op("dve", "tensor_tensor", [BK[dc // 2], b_mod], [b_gl[g_]], out=v3(gl_sb[g_][:, :n_], 8),
                               in0=v3(PS[:, dc * 256:dc * 256 + n_], 8), in1=bc_seq(modL[:, 40 + dc, 1:17], 8), op=ALU.mult)
                            op("dve", "tensor_tensor", [b_gl[g_]] + xb_, xb_, out=xT[:, dc, s0_:s0_ + n_], in0=gl_sb[g_][:, :n_],
                               in1=xT[:, dc, s0_:s0_ + n_], op=ALU.add)

            load_tab(0)
            emitA(0)
            for u in range(len(units)):
                if u + 1 < len(units):
                    emitA(u + 1)
                emitV(u)
            S.barrier()
            continue

        pend = deferred_front(0)
        emit(pend, len(pend))
        cpend = []
        for t in range(NTILE):
            pend = cpend + (deferred_front(t + 1) if t + 1 < NTILE else [])
            phaseB(t, pend)
            rec[0] = []
            phaseC(t)
            cpend = rec[0]
            rec[0] = None
        emit(cpend, len(cpend))
        S.barrier()

    S.barrier()
    frg = [0]
    for t in range(NTILE):
        s0 = t * 128
        xb = [b_x[t]]
        rms_rstd(xT[:, :, s0:s0 + 128], 128, xb, psq, b_psq, prstd, b_prstd, prtmp, b_prtmp, 0)
        for cc in range(8):
            op("dve", "scalar_tensor_tensor", xb + [b_pt, b_prstd], [b_hn2], out=hn2T[:, cc, :], in0=xT[:, cc, s0:s0 + 128],
               scalar=PT_B[:, 0, 24 + cc:25 + cc], in1=prstd[:], op0=ALU.mult, op1=ALU.mult)
        pb = 2 + (t % 2) * 2
        for cc in range(8):
            op("pe", "transpose", [b_hn2, b_ident], [BK[pb + cc // 4]], out=PS[:, pb * 512 + cc * 128: pb * 512 + (cc + 1) * 128],
               in_=hn2T[:, cc, :], identity=ident[:])
        sl = frg[0] % NG
        frg[0] += 1
        op("act", "copy", [BK[pb], BK[pb + 1]], [b_ring[sl]], out=ring[sl][:], in_=PS[:, pb * 512: pb * 512 + 1024])
        ob = Buf("y%d" % t)
        O_bufs.append(ob)
        S.dma("sp", "yout%d" % sl, [b_ring[sl]], [ob], out=y_d[s0:s0 + 128, :], in_=ring[sl][:])
    S.wait_all("sp", O_bufs)
    S.barrier()
    return nc, S, (mix_end, peer_end)


def host_prep(inp):
    f = np.float32
    g = {k: np.asarray(v) for k, v in inp.items()}
    common = {}
    common["w_ada"] = np.ascontiguousarray(g["w_ada"], f)
    common["w_in"] = np.ascontiguousarray(g["w_in"], f)
    common["w_out"] = np.ascontiguousarray(g["w_out"], f)
    common["w_q"] = np.ascontiguousarray(g["peer_w_q"], f)
    sk = g["peer_sub_keys"].astype(f)
    common["skT"] = np.ascontiguousarray(sk.reshape(L, 16, 128, 128).transpose(0, 1, 3, 2))
    pA = np.zeros((L, 128, 128), f)
    pB = np.zeros((L, 32, 128), f)
    for l in range(L):
        pA[l, 0:48] = g["b_ada"][l].reshape(48, 128)
        pA[l, 48:56] = g["norm1"][l].reshape(8, 128)
        pA[l, 56:64] = g["norm2"][l].reshape(8, 128)
        pA[l, 64:96] = g["conv_a_w"][l].reshape(32, 128)
        pA[l, 96:104] = g["conv_a_b"][l].reshape(8, 128)
        pA[l, 104:112] = g["rg_b_a"][l].reshape(8, 128)
        pA[l, 112:120] = g["rg_b_x"][l].reshape(8, 128)
        pA[l, 120:128] = g["rg_lambda"][l].reshape(8, 128)
        pB[l, 0:24] = g["conv_b_w"][l].reshape(24, 128)
        pB[l, 24:32] = g["final_norm"].reshape(8, 128)
    common["pA"] = pA
    common["pB"] = pB
    for nm, src in (("wbda", g["rg_w_a"]), ("wbdx", g["rg_w_x"])):
        w = np.zeros((L, 8, 128, 128), f)
        for c in range(8):
            w[:, c, 0:64, 0:64] = src[:, 2 * c]
            w[:, c, 64:128, 64:128] = src[:, 2 * c + 1]
        common[nm] = w
    common["ident"] = np.eye(128, dtype=f)
    common["iota16"] = np.tile(np.arange(16, dtype=f), (128, 1))
    if DENSE:
        common["iota128"] = np.tile(np.arange(128, dtype=f), (128, 1))
    for l in range(L):
        if DENSE:
            common["ut%d" % l] = np.ascontiguousarray(g["peer_u"][l].T, f)
        else:
            common["u%d" % l] = np.ascontiguousarray(g["peer_u"][l], f)
        common["v%d" % l] = np.ascontiguousarray(g["peer_v"][l], f)
    in_maps = []
    for i in range(8):
        m = dict(common)
        m["xin"] = np.ascontiguousarray(np.concatenate(
            [g["x_prompt"][i], g["x_sample"][16 * i:16 * i + 16].reshape(128, D)], axis=0), f)
        m["cin"] = np.ascontiguousarray(np.concatenate([g["c_prompt"][i:i + 1], g["c_sample"][16 * i:16 * i + 16]], axis=0), f)
        st = np.concatenate([g["state_conv_a"][:, 16 * i:16 * i + 16].reshape(L, 48, D),
                             g["state_h"][:, 16 * i:16 * i + 16],
                             g["state_conv_b"][:, 16 * i:16 * i + 16].reshape(L, 32, D)], axis=1)
        m["state"] = np.ascontiguousarray(st, f)
        in_maps.append(m)
    return in_maps


def assemble(results):
    f = np.float32
    y_p = np.zeros((8, 2048, D), f)
    y_s = np.zeros((128, 8, D), f)
    ca_p = np.zeros((L, 8, 3, D), f)
    h_p = np.zeros((L, 8, D), f)
    cb_p = np.zeros((L, 8, 2, D), f)
    ca_s = np.zeros((L, 128, 3, D), f)
    h_s = np.zeros((L, 128, D), f)
    cb_s = np.zeros((L, 128, 2, D), f)
    for i, r in enumerate(results):
        y = r["y"]
        y_p[i] = y[0:2048]
        y_s[16 * i:16 * i + 16] = y[2048:].reshape(16, 8, D)
        o = r["ost"]
        ca_s[:, 16 * i:16 * i + 16] = o[:, 0:48].reshape(L, 16, 3, D)
        h_s[:, 16 * i:16 * i + 16] = o[:, 48:64]
        cb_s[:, 16 * i:16 * i + 16] = o[:, 64:96].reshape(L, 16, 2, D)
        ca_p[:, i] = o[:, 96:99]
        h_p[:, i] = o[:, 99]
        cb_p[:, i] = o[:, 100:102]
    return (y_p, y_s, ca_p, h_p, cb_p, ca_s, h_s, cb_s)


def kernel(**inputs):
    in_maps = host_prep(inputs)
    nc, S, _ = build_program()
    res = run_bass_kernel_spmd(nc, in_maps, core_ids=list(range(8)))
    return assemble(res.results)
```
